# Optimizing a Trainium2 kernel written in Bass

```python
import math
import jax, jax.numpy as jnp
from jax import lax
import numpy as np

D_MODEL = 1024
BATCH = 2
SEQ = 8192
DEPTH = 1

CHUNK = 64
GMLP_BLOCK = 128
SB_QBLOCK = 128
MIX_WIDTH = D_MODEL
WIDTH_A = MIX_WIDTH // 2
WIDTH_B = MIX_WIDTH - WIDTH_A
HEAD_DIM = 64
HEADS_A = WIDTH_A // HEAD_DIM
HEADS_B = WIDTH_B // HEAD_DIM
IN_WIDTH = 3 * WIDTH_A + 4 * WIDTH_B
DEEPNORM_ALPHA = (2 * DEPTH) ** 0.25
DEEPNORM_BETA = (8 * DEPTH) ** -0.25
LN_EPS = 1e-5

kernel_name = "hybrid_gmlp_stickbreaking_deepnorm_adaln"


def layer_norm(x, g, b):
    xf = x.astype(jnp.float32)
    mu = jnp.mean(xf, axis=-1, keepdims=True)
    var = jnp.mean(jnp.square(xf - mu), axis=-1, keepdims=True)
    y = (xf - mu) * lax.rsqrt(var + LN_EPS)
    return (y * g.astype(jnp.float32) + b.astype(jnp.float32)).astype(x.dtype)


def chunk_causal_block_mask():
    i = jnp.arange(GMLP_BLOCK) // CHUNK
    return i[None, :] <= i[:, None]


def gmlp_spatial_gating(u, v, ln_g, ln_b, w_s, b_s):
    B, S, _ = v.shape
    nblk = S // GMLP_BLOCK
    v = layer_norm(v, ln_g, ln_b)
    vb = v.reshape(B, nblk, GMLP_BLOCK, HEADS_A, HEAD_DIM)
    w = jnp.where(chunk_causal_block_mask()[None], w_s, jnp.zeros_like(w_s))
    mixed = jnp.einsum('hij,bnjhd->bnihd', w, vb) + jnp.transpose(b_s)[None, None, :, :, None]
    return u * mixed.reshape(B, S, WIDTH_A)


def stick_breaking_attention(q, k, v):
    S = q.shape[2]
    dh = q.shape[3]
    scale = 1.0 / math.sqrt(dh)
    outs = []
    for qb in range(S // SB_QBLOCK):
        start = qb * SB_QBLOCK
        end = start + SB_QBLOCK
        kk = k[:, :, :end]
        vv = v[:, :, :end]
        z = jnp.einsum('bhqd,bhkd->bhqk', q[:, :, start:end], kk,
                       preferred_element_type=jnp.float32) * scale
        t_pos = start + jnp.arange(SB_QBLOCK)[:, None]
        s_pos = jnp.arange(end)[None, :]
        valid = s_pos < t_pos
        log_keep = jnp.where(valid, -jax.nn.softplus(z), 0.0)
        after = lax.cumsum(log_keep, axis=3, reverse=True) - log_keep
        log_w = jax.nn.log_sigmoid(z) + after
        w = jnp.where(valid, jnp.exp(log_w), 0.0)
        outs.append(jnp.einsum('bhqk,bhkd->bhqd', w.astype(v.dtype), vv))
    return jnp.concatenate(outs, axis=2)


def setup_inputs(seed: int = 0) -> dict:
    key = jax.random.key(seed)
    ks = jax.random.split(key, 12)
    f32 = jnp.float32
    x = jax.random.normal(ks[0], (BATCH, SEQ, D_MODEL), f32)
    c = jax.random.normal(ks[1], (BATCH, D_MODEL), f32)
    w_ada = jax.random.normal(ks[2], (DEPTH, D_MODEL, 3 * D_MODEL), f32) * (0.1 * D_MODEL ** -0.5)
    b_ada = jax.random.normal(ks[3], (DEPTH, 3 * D_MODEL), f32) * 0.02
    w_in = jax.random.normal(ks[4], (DEPTH, D_MODEL, IN_WIDTH), f32) * D_MODEL ** -0.5
    ln_v_g = 1.0 + 0.02 * jax.random.normal(ks[5], (DEPTH, WIDTH_A), f32)
    ln_v_b = 0.02 * jax.random.normal(ks[6], (DEPTH, WIDTH_A), f32)
    w_spatial = jax.random.normal(ks[7], (DEPTH, HEADS_A, GMLP_BLOCK, GMLP_BLOCK), f32) * GMLP_BLOCK ** -0.5
    b_spatial = 1.0 + 0.02 * jax.random.normal(ks[8], (DEPTH, HEADS_A, GMLP_BLOCK), f32)
    w_out = jax.random.normal(ks[9], (DEPTH, MIX_WIDTH, D_MODEL), f32) * (MIX_WIDTH ** -0.5 * DEEPNORM_BETA)
    ln_g = 1.0 + 0.02 * jax.random.normal(ks[10], (DEPTH, D_MODEL), f32)
    ln_b = 0.02 * jax.random.normal(ks[11], (DEPTH, D_MODEL), f32)
    return {"x": x, "c": c, "w_ada": w_ada, "b_ada": b_ada, "w_in": w_in,
            "ln_v_g": ln_v_g, "ln_v_b": ln_v_b, "w_spatial": w_spatial, "b_spatial": b_spatial,
            "w_out": w_out, "ln_g": ln_g, "ln_b": ln_b}


def reference(x, c, w_ada, b_ada, w_in, ln_v_g, ln_v_b, w_spatial, b_spatial, w_out, ln_g, ln_b):
    B, S, D = x.shape
    for l in range(DEPTH):
        mod = c @ w_ada[l] + b_ada[l]
        shift, scale, gate = jnp.split(mod, 3, axis=-1)
        h = x * (1.0 + scale[:, None, :]) + shift[:, None, :]
        proj = h @ w_in[l]
        u_a, v_a, z_a, q_b, k_b, v_b, z_b = jnp.split(
            proj, np.cumsum([WIDTH_A, WIDTH_A, WIDTH_A, WIDTH_B, WIDTH_B, WIDTH_B]).tolist(), axis=-1)
        out_a = gmlp_spatial_gating(jax.nn.gelu(u_a), jax.nn.gelu(v_a),
                                    ln_v_g[l], ln_v_b[l], w_spatial[l], b_spatial[l])
        out_a = out_a * jax.nn.silu(z_a)
        to_heads = lambda t: jnp.transpose(t.reshape(B, S, HEADS_B, HEAD_DIM), (0, 2, 1, 3))
        o_b = stick_breaking_attention(to_heads(q_b), to_heads(k_b), to_heads(v_b))
        out_b = jnp.transpose(o_b, (0, 2, 1, 3)).reshape(B, S, WIDTH_B) * jax.nn.silu(z_b)
        y = jnp.concatenate([out_a, out_b], axis=-1) @ w_out[l]
        x = layer_norm(DEEPNORM_ALPHA * x + (1.0 + gate[:, None, :]) * y, ln_g[l], ln_b[l])
    return x
```

```python
import contextlib
import numpy as np
import concourse.bass as bass
import concourse.mybir as mybir
from concourse.bass_utils import run_bass_kernel_spmd

F32 = mybir.dt.float32
BF16 = mybir.dt.bfloat16
ALU = mybir.AluOpType
AF = mybir.ActivationFunctionType

S = 8192
DM = 1024
NOWN = 2048
ALPHA = 2.0 ** 0.25
LN_EPS = 1e-5
ARENA_WORDS = 51200
N_DSEM = 40
STAGE = 99


class Prog:
    def __init__(self, cur, eobj, sems, dsems):
        self.cur = cur
        self.e = eobj
        self.sems = sems
        self.dsems = dsems
        self.cnt = {k: 0 for k in sems}
        self.waited = {}
        self.dmap = {}
        self.dcnt = {}

    def _sem(self, key):
        if key in self.sems:
            return self.sems[key]
        return self.dsems[self.dmap[key]]

    def wait(self, eng, toks):
        for t in toks:
            if t is None:
                continue
            key, val = t
            if val > self.waited.get((eng, key), 0):
                self.waited[(eng, key)] = val
                if eng == self.cur:
                    self.e.wait_ge(self._sem(key), val)

    def op(self, eng, fn, waits=(), sig=True):
        self.wait(eng, waits)
        tok = None
        if sig:
            self.cnt[eng] += 1
            tok = (eng, self.cnt[eng])
        if eng == self.cur:
            ins = fn(self.e)
            if sig:
                ins.then_inc(self.sems[eng], 1)
        return tok

    def dma(self, slot, out, in_, waits=(), q="sp"):
        self.wait(q, waits)
        if slot not in self.dmap:
            self.dmap[slot] = len(self.dmap)
            self.dcnt[slot] = 0
            assert len(self.dmap) <= len(self.dsems), "out of dma semaphores"
        self.dcnt[slot] += 16
        if q == self.cur:
            self.e.dma_start(out=out, in_=in_).then_inc(self.dsems[self.dmap[slot]], 16)
        return (slot, self.dcnt[slot])


class Arena:
    def __init__(self, ap):
        self.ap = ap
        self.off = 0
        self.hi = 0

    def mark(self):
        return self.off

    def reset(self, off):
        self.off = off

    def alloc(self, shape_free, dtype, parts=128):
        esz = 2 if dtype == BF16 else 4
        n = int(np.prod(shape_free)) * esz
        n = (n + 31) // 32 * 32
        off = self.off
        self.off += n
        self.hi = max(self.hi, self.off)
        assert self.off <= ARENA_WORDS * 4, f"arena overflow {self.off}"
        a = self.ap[0:parts, off // 4:(off + n) // 4]
        if dtype == BF16:
            a = a.bitcast(BF16)
        tot = int(np.prod(shape_free))
        a = a[:, 0:tot]
        if len(shape_free) == 2:
            a = a.rearrange("p (a b) -> p a b", b=shape_free[1])
        elif len(shape_free) == 3:
            a = a.rearrange("p (a b c) -> p a b c", b=shape_free[1], c=shape_free[2])
        return a


def generate(P, nc, T, arena_ap, psb):
    A = Arena(arena_ap)
    op = P.op
    dma = P.dma

    def psbf(b):
        return psb[b].bitcast(BF16)

    ident = A.alloc([128], BF16)
    zeros = A.alloc([512], F32)
    dmat = A.alloc([512], F32)
    sc1 = A.alloc([8], F32)
    sh = A.alloc([8], F32)
    epst = A.alloc([8], F32)
    ones_r = A.alloc([128], F32)
    mhalf = A.alloc([8], F32)
    obT = A.alloc([4, NOWN], BF16)
    G1 = A.alloc([1024], F32)
    base_mark = A.mark()
    wu = A.alloc([8, 1536], BF16)
    wo = A.alloc([8, 1024], BF16)
    wst3 = [A.alloc([1536], F32), A.alloc([1536], F32)]
    p3_head_end = A.mark()
    A.reset(base_mark)

    A.reset(base_mark)
    wh = A.alloc([8, 1024], BF16)
    wsx = A.alloc([2112], F32)
    wstage = [wsx[:, 0:1024], wsx[:, 1024:2048]]
    xs = [A.alloc([8, 258], F32), A.alloc([8, 258], F32),
          wsx[:, 0:2064].rearrange("p (a b) -> p a b", b=258)]
    hTt = [A.alloc([8, 258], BF16), A.alloc([8, 258], BF16)]
    vtmp = [A.alloc([2, 258], F32), A.alloc([2, 258], F32)]
    gTt = [A.alloc([2, 256], BF16), A.alloc([2, 256], BF16)]
    assert A.mark() >= p3_head_end, (A.mark(), p3_head_end)
    setup_mark = A.mark()
    kT = A.alloc([2, S], BF16)
    g = A.alloc([64, 256], BF16)
    qT = A.alloc([2, NOWN], BF16)
    vown = A.alloc([2, NOWN], F32)
    szb = A.alloc([2, NOWN], BF16)
    pbuf = [[A.alloc([512], F32) for _ in range(2)] for _ in range(2)]
    Ebuf = [[A.alloc([512], BF16) for _ in range(2)] for _ in range(3)]
    ETbuf = [[A.alloc([512], BF16) for _ in range(2)] for _ in range(2)]
    eptmp = [A.alloc([128], F32), A.alloc([128], F32)]

    xT_all = T["xT_all"].rearrange("(c p) t -> p c t", p=128)
    xT_own = T["xT_own"].rearrange("(c p) t -> p c t", p=128)

    st = {"mod_done": [None, None, None], "pe_done": [None, None], "gt_rd": [None, None],
          "vt_rd": [None, None], "bank_free": [None] * 8, "cast": None, "tile": 0}

    def load_half_weights(half, cast_waits=(), split_cast=False):
        cast_tok = None
        for kc in range(8):
            sb = kc % 2
            tw = None
            for gi, c0 in enumerate((2048, 2560, 1536, 3072)):
                tw = dma(f"wst{sb}", wstage[sb][:, gi * 256:(gi + 1) * 256],
                         T["w_in"][kc * 128:(kc + 1) * 128, c0 + half * 256:c0 + half * 256 + 256],
                         waits=[st.get(f"wcast{sb}")] + list(st["mod_done"][2] or []))
            if split_cast:
                cast_tok = op("act", lambda e, kc=kc, sb=sb: e.activation(
                    out=wh[:, kc, :], in_=wstage[sb], func=AF.Identity), waits=[tw] + list(cast_waits))
            else:
                cast_tok = op("pool", lambda e, kc=kc, sb=sb: e.tensor_copy(out=wh[:, kc, :], in_=wstage[sb]),
                              waits=[tw] + list(cast_waits))
            st[f"wcast{sb}"] = cast_tok
        st["cast"] = st["wcast1"]
        st["cast2"] = st["wcast0"]


    frame_end = A.mark()
    A.reset(setup_mark)
    ident32 = A.alloc([128], F32)
    cT = A.alloc([8], F32)
    bada = A.alloc([3072], F32, parts=1)
    modrow = A.alloc([3072], F32, parts=1)
    NWB = 6
    wst = [A.alloc([3072], F32) for _ in range(NWB)]

    t_id = dma("c_ident", ident32, T["ident"][:, :])
    t_dm = dma("c_dmat", dmat, T["dmat"][:, :])
    t_c = dma("c_cT", cT, T["cT"][:, :])
    t_ba = dma("c_bada", bada, T["b_ada"][0:1, :])
    k_z = op("dve", lambda e: e.memset(zeros, 0.0))
    k_one = op("dve", lambda e: e.memset(ones_r, 1.0))
    k_eps = op("dve", lambda e: e.memset(epst, LN_EPS))
    k_mh = op("dve", lambda e: e.memset(mhalf, -0.5))
    k_ident = op("dve", lambda e: e.tensor_copy(out=ident, in_=ident32), waits=[t_id])

    pe_last = [None] * NWB
    mm_tok = None
    for kc in range(8):
        wb = kc % NWB
        t_w = dma(f"wada{wb}", wst[wb], T["w_ada"][kc * 128:(kc + 1) * 128, :], waits=[pe_last[wb]])
        for ct in range(6):
            mm_tok = op("pe", lambda e, kc=kc, ct=ct, wb=wb: e.matmul(
                psb[ct][0:1, 0:512], lhsT=cT[:, kc:kc + 1], rhs=wst[wb][:, ct * 512:(ct + 1) * 512],
                start=(kc == 0), stop=(kc == 7)), waits=[t_w, t_c], sig=(ct == 5))
        pe_last[wb] = mm_tok
    pre_d = {}
    for n in range(2):
        pre_d[n] = (n, 257, dma(f"xs{n}", xs[n][:, :, 0:257], xT_all[:, :, n * 256:n * 256 + 257]))
    st["dtile"] = 2
    load_half_weights(0, split_cast=True)
    k_mod = None
    for ct in range(6):
        k_mod = op("dve", lambda e, ct=ct: e.tensor_tensor(
            out=modrow[:, ct * 512:(ct + 1) * 512], in0=psb[ct][0:1, 0:512],
            in1=bada[:, ct * 512:(ct + 1) * 512], op=ALU.add), waits=[mm_tok, t_ba])
    t_gsc = dma("gate_sc", T["gate_scratch"][0:1, :], modrow[:, 2048:3072], waits=[k_mod])
    tp = None
    for i in range(16):
        tp = op("pe", lambda e, i=i: e.matmul(
            psb[6][:, i:i + 1], lhsT=modrow[:, i * 128:(i + 1) * 128], rhs=ones_r[0:1, 0:1],
            start=True, stop=True), waits=[k_mod, k_one], sig=(i == 15))
    k_sh = op("dve", lambda e: e.tensor_copy(out=sh, in_=psb[6][:, 0:8]), waits=[tp])
    k_sc = op("dve", lambda e: e.tensor_scalar(out=sc1, in0=psb[6][:, 8:16], scalar1=1.0, scalar2=None,
                                               op0=ALU.add), waits=[tp])
    setup_done = [k_sc, k_sh, k_ident, k_z, k_eps, k_mh, t_dm, t_gsc, tp]
    assert A.mark() <= frame_end
    for eng in ("pe", "act", "dve", "pool"):
        P.wait(eng, setup_done)
    if STAGE == 0:
        return

    def modulate(buf, xb, ncols, t_x):
        toks = []
        for c in range(8):
            w = [t_x, st["pe_done"][buf]]
            if c < 5:
                toks.append(op("act", lambda e, c=c: e.activation(
                    out=hTt[buf][:, c, 0:ncols], in_=xs[xb][:, c, 0:ncols], func=AF.Identity,
                    scale=sc1[:, c:c + 1], bias=sh[:, c:c + 1]), waits=w))
            else:
                toks.append(op("pool", lambda e, c=c: e.tensor_scalar(
                    out=hTt[buf][:, c, 0:ncols], in0=xs[xb][:, c, 0:ncols],
                    scalar1=sc1[:, c:c + 1], scalar2=sh[:, c:c + 1], op0=ALU.mult, op1=ALU.add), waits=w))
        st["mod_done"][xb] = [toks[4], toks[7]]
        return [toks[4], toks[7]]

    def proj(buf, bank, col0, ncols, mtoks):
        tok = None
        for kc in range(8):
            tok = op("pe", lambda e, kc=kc: e.matmul(
                psb[bank][:, 0:ncols], lhsT=wh[:, kc, col0:col0 + 128], rhs=hTt[buf][:, kc, 0:ncols],
                start=(kc == 0), stop=(kc == 7)),
                waits=list(mtoks) + [st["bank_free"][bank], st["cast"], st.get("cast2")], sig=(kc == 7))
        return tok

    for half in range(2):
        tiles = [("kv", tt) for tt in range(32)] + [("own", ot) for ot in range(8)]
        tinfo = {}

        dinfo = {}

        def stage_D(n):
            kind, idx = tiles[n]
            xb = st.setdefault("dtile", 0) % 3
            st["dtile"] += 1
            if kind == "kv":
                ncols = 257
                src = xT_all[:, :, idx * 256:idx * 256 + 257]
            else:
                ncols = 256
                src = xT_own[:, :, idx * 256:(idx + 1) * 256]
            w = list(st["mod_done"][xb] or [])
            if xb == 2:
                w += [st.get("wcast0"), st.get("wcast1")]
            dinfo[n] = (xb, ncols, dma(f"xs{xb}", xs[xb][:, :, 0:ncols], src, waits=w))

        def stage_M(n):
            buf = st["tile"] % 2
            st["tile"] += 1
            xb, ncols, t_x = dinfo[n]
            mt = modulate(buf, xb, ncols, t_x)
            tinfo[n] = (buf, mt)

        def stage_C1(n):
            kind, idx = tiles[n]
            buf, mt = tinfo[n]
            pe_tok = None
            if kind == "kv":
                tt = idx
                for hp in range(2):
                    pe_tok = proj(buf, hp, hp * 128, 256, mt)
                    kk = op("dve", lambda e, hp=hp: e.tensor_copy(
                        out=kT[:, hp, tt * 256:(tt + 1) * 256], in_=psb[hp][:, 0:256]),
                        waits=[pe_tok, st.get("att_pe_done")])
                    st["bank_free"][hp] = kk
                vts = []
                for hp in range(2):
                    pe_tok = proj(buf, 2 + hp, 256 + hp * 128, 257, mt)
                    vk = op("dve", lambda e, hp=hp: e.tensor_copy(
                        out=vtmp[buf][:, hp, 0:257], in_=psb[2 + hp][:, 0:257]),
                        waits=[pe_tok, st["vt_rd"][buf]])
                    st["bank_free"][2 + hp] = vk
                    vts.append(vk)
                st["pe_done"][buf] = pe_tok
                if tt == 31:
                    vts.append(op("dve", lambda e: e.memset(vtmp[buf][:, :, 256:257], 0.0), waits=vts))
                gk = op("pool", lambda e: e.tensor_tensor(
                    out=gTt[buf][:, :, :], in0=vtmp[buf][:, :, 1:257], in1=vtmp[buf][:, :, 0:256],
                    op=ALU.subtract), waits=vts + [st["gt_rd"][buf]])
                st["vt_rd"][buf] = gk
                st["k_done"] = st["bank_free"][1]
                tinfo[n] = (buf, mt, gk)
            else:
                ot = idx
                for hp in range(2):
                    pe_tok = proj(buf, hp, 512 + hp * 128, 256, mt)
                    kk = op("dve", lambda e, hp=hp: e.tensor_copy(
                        out=qT[:, hp, ot * 256:(ot + 1) * 256], in_=psb[hp][:, 0:256]),
                        waits=[pe_tok, st.get("att_pe_done")])
                    st["bank_free"][hp] = kk
                for hp in range(2):
                    pe_tok = proj(buf, 2 + hp, 256 + hp * 128, 256, mt)
                    kk = op("dve", lambda e, hp=hp: e.tensor_copy(
                        out=vown[:, hp, ot * 256:(ot + 1) * 256], in_=psb[2 + hp][:, 0:256]),
                        waits=[pe_tok, st.get("att_ep_done")])
                    st["bank_free"][2 + hp] = kk
                for hp in range(2):
                    pe_tok = proj(buf, 6 + hp, 768 + hp * 128, 256, mt)
                    kk = op("act", lambda e, hp=hp: e.activation(
                        out=szb[:, hp, ot * 256:(ot + 1) * 256], in_=psb[6 + hp][:, 0:256], func=AF.Silu),
                        waits=[pe_tok, st.get("att_ep_done")])
                    st["bank_free"][6 + hp] = kk
                st["pe_done"][buf] = pe_tok
                st["q_done"] = [st["bank_free"][0], st["bank_free"][1], st["bank_free"][2],
                                st["bank_free"][3], st["bank_free"][6], st["bank_free"][7]]

        def stage_C2(n):
            kind, idx = tiles[n]
            if kind != "kv":
                return
            tt = idx
            buf, mt, gk = tinfo[n]
            tb = 4 + buf
            trt = None
            for blk in range(2):
                for hp in range(2):
                    trt = op("pe", lambda e, blk=blk, hp=hp: e.transpose(
                        psbf(tb)[:, (blk * 2 + hp) * 128:(blk * 2 + hp + 1) * 128],
                        gTt[buf][:, hp, blk * 128:(blk + 1) * 128], ident),
                        waits=[gk, st["bank_free"][tb]], sig=(blk == 1 and hp == 1))
            st["gt_rd"][buf] = trt
            gev = op("act", lambda e: e.activation(
                out=g[:, tt * 2:tt * 2 + 2, :], in_=psbf(tb)[:, 0:512].rearrange("p (a b) -> p a b", b=256),
                func=AF.Identity), waits=[trt, st.get("att_pe_done")])
            st["bank_free"][tb] = gev
            st["g_done"] = gev

        NTL = len(tiles)
        if half == 0:
            dinfo.update(pre_d)
        else:
            stage_D(0)
            stage_D(1)
        stage_M(0)
        for n in range(NTL):
            if n + 2 < NTL:
                stage_D(n + 2)
            if n + 1 < NTL:
                stage_M(n + 1)
            stage_C1(n)
            if n >= 1:
                stage_C2(n - 1)
        stage_C2(NTL - 1)

        if STAGE == 2:
            for eng in ("pe", "act", "dve", "pool", "sp"):
                P.wait(eng, list(st["q_done"]) + [st["pe_done"][0], st["pe_done"][1]])
            return
        if half == 0:
            load_half_weights(1, cast_waits=[st["pe_done"][0], st["pe_done"][1]])
        if half == 1:
            ph1_done = [st["pe_done"][0], st["pe_done"][1], st["gt_rd"][0], st["gt_rd"][1], st["vt_rd"][0],
                        st["vt_rd"][1], st["wcast0"], st["wcast1"], st["g_done"]] + \
                list(st["mod_done"][0] or []) + list(st["mod_done"][1] or []) + \
                list(st["mod_done"][2] or []) + list(st["q_done"])
            t_g1 = dma("p3_g1", G1, T["gate_scratch"][0:1, :].partition_broadcast(128), waits=[t_gsc])
            k_g1 = op("pool", lambda e: e.tensor_scalar(out=G1, in0=G1, scalar1=1.0, scalar2=None, op0=ALU.add),
                      waits=[t_g1])
            wtok = {}
            cu = None
            for kc in range(8):
                sb = kc % 2
                tw = dma(f"w3st{sb}", wst3[sb], T["w_in"][kc * 128:(kc + 1) * 128, 0:1536],
                         waits=ph1_done + [wtok.get(sb)])
                cu = op("pool", lambda e, kc=kc, sb=sb: e.tensor_copy(out=wu[:, kc, :], in_=wst3[sb]),
                        waits=[tw] + ph1_done)
                wtok[sb] = cu
            co = None
            for kc in range(8):
                sb = kc % 2
                tw = dma(f"w3st{sb}", wst3[sb][:, 0:1024], T["w_out"][kc * 128:(kc + 1) * 128, :],
                         waits=[wtok.get(sb)])
                co = op("pool", lambda e, kc=kc, sb=sb: e.tensor_tensor(
                    out=wo[:, kc, :], in0=wst3[sb][:, 0:1024], in1=G1, op=ALU.mult), waits=[tw, k_g1])
                wtok[sb] = co
        pus = [(m, hp, k) for m in range(16) for hp in range(2) for k in range(m, 16)]
        NPU = len(pus)
        ready = list(st["q_done"]) + [st["g_done"], st["k_done"], st["pe_done"][0], st["pe_done"][1],
                                      st["bank_free"][4], st["bank_free"][5]]
        tk = {}

        def qk(i):
            m, hp, k = pus[i]
            par = i % 2
            for hl in range(2):
                tk[("qk", i, hl)] = op("pe", lambda e, hl=hl: e.matmul(
                    psb[par * 2 + hl][:, 0:512],
                    lhsT=qT[hl * 64:(hl + 1) * 64, hp, m * 128:(m + 1) * 128],
                    rhs=kT[hl * 64:(hl + 1) * 64, hp, k * 512:(k + 1) * 512],
                    start=True, stop=True), waits=ready + [tk.get(("sig", i - 2, hl))])

        def sig(i):
            par = i % 2
            for hl in range(2):
                tk[("sig", i, hl)] = op("act", lambda e, hl=hl: e.activation(
                    out=pbuf[par][hl], in_=psb[par * 2 + hl][:, 0:512], func=AF.Sigmoid, scale=-0.125),
                    waits=[tk[("qk", i, hl)], tk.get(("scan", i - 2, hl))])

        def scan(i):
            m, hp, k = pus[i]
            par = i % 2
            for hl in range(2):
                first = (k == m)
                w = [tk[("sig", i, hl)], tk.get(("tr", i - 3, hl)), tk.get(("scan", i - 2, hl))]
                if not first:
                    w.append(tk[("scan", i - 1, hl)])
                    init = Ebuf[(i - 1) % 3][hl][:, 511:512]
                    d1 = zeros
                else:
                    init = 0.0
                    d1 = dmat
                tk[("scan", i, hl)] = op("dve", lambda e, hl=hl, init=init, d1=d1: e.tensor_tensor_scan(
                    out=Ebuf[i % 3][hl], data0=pbuf[par][hl], data1=d1, initial=init,
                    op0=ALU.mult, op1=ALU.add), waits=w)

        def tr(i):
            par = i % 2
            for hl in range(2):
                t = None
                for blk in range(4):
                    t = op("pe", lambda e, hl=hl, blk=blk: e.transpose(
                        psbf(4 + par)[:, hl * 512 + blk * 128:hl * 512 + (blk + 1) * 128],
                        Ebuf[i % 3][hl][:, blk * 128:(blk + 1) * 128], ident),
                        waits=[tk[("scan", i, hl)], tk.get(("ev", i - 2, 0)), tk.get(("ev", i - 2, 1))],
                        sig=(blk == 3))
                tk[("tr", i, hl)] = t

        def ev(i):
            par = i % 2
            for hl in range(2):
                tk[("ev", i, hl)] = op("act", lambda e, hl=hl: e.activation(
                    out=ETbuf[par][hl], in_=psbf(4 + par)[:, hl * 512:(hl + 1) * 512], func=AF.Identity),
                    waits=[tk[("tr", i, 0)], tk[("tr", i, 1)], tk.get(("wv", i - 2))])

        def wv(i):
            m, hp, k = pus[i]
            par = i % 2
            chain = m * 2 + hp
            ob = 6 + chain % 2
            while pend and pend[0][1] <= chain - 2:
                epilogue(pend.pop(0))
            t = None
            for blk in range(4):
                for hl in range(2):
                    t = op("pe", lambda e, hl=hl, blk=blk: e.matmul(
                        psb[ob][hl * 64:(hl + 1) * 64, 0:128],
                        lhsT=g[:, k * 4 + blk, hp * 128 + hl * 64:hp * 128 + (hl + 1) * 64],
                        rhs=ETbuf[par][hl][:, blk * 128:(blk + 1) * 128],
                        start=(k == m and blk == 0), stop=(k == 15 and blk == 3)),
                        waits=[tk[("ev", i, hl)], tk.get(("ep", chain - 2))], sig=(blk == 3 and hl == 1))
            tk[("wv", i)] = t
            if k == 15:
                pend.append((i, chain, m, hp, ob, t))

        pend = []

        def epilogue(ent):
            i, chain, m, hp, ob, t = ent
            e1 = op("dve", lambda e: e.tensor_tensor(
                out=eptmp[chain % 2], in0=psb[ob][:, 0:128], in1=vown[:, hp, m * 128:(m + 1) * 128],
                op=ALU.add), waits=[t, tk.get(("ep2", chain - 2))])
            tk[("ep", chain)] = e1
            tk[("ep2", chain)] = op("pool", lambda e: e.tensor_tensor(
                out=obT[:, half * 2 + hp, m * 128:(m + 1) * 128], in0=eptmp[chain % 2],
                in1=szb[:, hp, m * 128:(m + 1) * 128], op=ALU.mult), waits=[e1])
            st["att_ep_done"] = tk[("ep2", chain)]

        def flush_ep(upto):
            while pend and pend[0][0] <= upto:
                epilogue(pend.pop(0))

        qk(0)
        if NPU > 1:
            qk(1)
        sig(0)
        for i in range(NPU):
            if i + 1 < NPU:
                sig(i + 1)
            scan(i)
            flush_ep(i - 3)
            tr(i)
            if i + 2 < NPU:
                qk(i + 2)
            ev(i)
            if i >= 1:
                wv(i - 1)
        wv(NPU - 1)
        flush_ep(NPU)
        st["att_pe_done"] = tk[("wv", NPU - 1)]
        st["pe_all"] = tk[("wv", NPU - 1)]
        for b in range(8):
            st["bank_free"][b] = st["att_ep_done"] if b >= 6 else tk[("ev", NPU - 1, 1)]

        if STAGE == 3:
            for eng in ("pe", "act", "dve", "pool", "sp"):
                P.wait(eng, [st["att_ep_done"], st["att_pe_done"], tk[("ev", NPU - 1, 1)]])
            return
    fin = [st["att_ep_done"], st["att_pe_done"], tk[("ev", NPU - 1, 1)], st["wcast0"], st["wcast1"]]
    for eng in ("pe", "act", "dve", "pool", "sp"):
        P.wait(eng, fin)

    A.reset(p3_head_end)
    lng = A.alloc([1024], F32)
    lnb = A.alloc([1024], F32)
    lnvg = A.alloc([512], F32)
    lnvb = A.alloc([512], F32)
    bsT = A.alloc([4, 128], F32)
    wsT = A.alloc([8, 128], BF16)
    ws32 = A.alloc([8, 128], F32)
    sel = A.alloc([4, 128], F32, parts=8)
    bs8 = A.alloc([128], F32, parts=8)
    xs3_1 = A.alloc([8, 512], F32)
    xs3 = [xs3_1, xs3_1]
    hT3 = [A.alloc([8, 512], BF16), A.alloc([8, 512], BF16)]
    uT = [A.alloc([4, 512], F32), A.alloc([4, 512], F32)]
    zaT_1 = A.alloc([4, 512], F32)
    zaT = [zaT_1, zaT_1]
    gv = [A.alloc([512], F32) for _ in range(3)]
    vnrm = A.alloc([512], F32)
    vn = [A.alloc([512], BF16) for _ in range(3)]
    mx = A.alloc([4, 128], F32)
    oaT = [A.alloc([4, 128], BF16), A.alloc([4, 128], BF16)]
    xtok = [A.alloc([1024], F32), A.alloc([1024], F32)]
    rr = A.alloc([1024], F32)
    yo = [A.alloc([1024], F32), A.alloc([1024], F32)]
    stats = A.alloc([32], F32)

    t_lng = dma("p3_lng", lng, T["ln_g"][0:1, :].partition_broadcast(128))
    t_lnb = dma("p3_lnb", lnb, T["ln_b"][0:1, :].partition_broadcast(128))
    t_lvg = dma("p3_lvg", lnvg, T["ln_v_g"][0:1, :].partition_broadcast(128))
    t_lvb = dma("p3_lvb", lnvb, T["ln_v_b"][0:1, :].partition_broadcast(128))
    t_ws = dma("p3_ws", ws32, T["wsT"].rearrange("h j i -> j h i"))
    t_sel = dma("p3_sel", sel, T["sel"])
    t_bs = dma("p3_bs", bs8, T["bs_rev"][:, :])
    k_ws = op("dve", lambda e: e.tensor_copy(out=wsT, in_=ws32), waits=[t_ws])
    k_ws = op("dve", lambda e: e.memset(wsT[0:64, :, 64:128], 0.0), waits=[k_ws])
    tb_ = None
    for hp in range(4):
        tb_ = op("pe", lambda e, hp=hp: e.matmul(
            psb[0][:, hp * 128:(hp + 1) * 128], lhsT=sel[:, hp, :], rhs=bs8[:, :], start=True, stop=True),
            waits=[t_sel, t_bs], sig=(hp == 3))
    k_bs = op("dve", lambda e: e.tensor_copy(out=bsT, in_=psb[0][:, 0:512].rearrange("p (a b) -> p a b", b=128)),
              waits=[tb_])
    s3 = {"hT_rd": [None, None], "xs_rd": [None, None], "ps_free": [k_bs] + [None] * 7,
          "uT_rd": [None, None], "gv_rd": [None, None, None], "vn_rd": [None, None, None], "oa_rd": [None, None],
          "x_rd": [None, None], "yo_rd": [None, None], "mx_rd": None, "rr_rd": None, "vnrm_rd": None,
          "stA_rd": None, "stC_rd": None}
    out_toks = []
    tile_mt = {}
    tile_uz = {}
    blk_oa = {}
    blk_vn = {}
    sa1 = {}
    scs = {}

    tl_dma = {}

    def TLd(Tt):
        tb = Tt % 2
        tl_dma[Tt] = dma("xs3", xs3[tb], xT_own[:, :, Tt * 512:(Tt + 1) * 512], waits=s3["xs_rd"][0] or [])

    def TL(Tt):
        tb = Tt % 2
        if Tt not in tl_dma:
            TLd(Tt)
        t_x = tl_dma[Tt]
        mtoks = []
        for c in range(8):
            w = [t_x, s3["hT_rd"][tb]]
            mtoks.append(op("act", lambda e, c=c: e.activation(
                out=hT3[tb][:, c, :], in_=xs3[tb][:, c, :], func=AF.Identity,
                scale=sc1[:, c:c + 1], bias=sh[:, c:c + 1]), waits=w))
        s3["xs_rd"][0] = [mtoks[4], mtoks[7]]
        tile_mt[Tt] = [mtoks[4], mtoks[7]]

    tu_toks = {}

    def TUp(Tt, part):
        tb = Tt % 2
        mt = tile_mt[Tt]
        c0, dst, fn = ((0, uT[tb], AF.Gelu_apprx_tanh), (1024, zaT[tb], AF.Silu))[part // 2]
        for fc in ((part % 2) * 2, (part % 2) * 2 + 1):
            bank = fc % 2
            t = None
            for kc in range(8):
                t = op("pe", lambda e, kc=kc, fc=fc, c0=c0: e.matmul(
                    psb[bank][:, 0:512], lhsT=wu[:, kc, c0 + fc * 128:c0 + (fc + 1) * 128],
                    rhs=hT3[tb][:, kc, :], start=(kc == 0), stop=(kc == 7)),
                    waits=mt + [cu, s3["ps_free"][bank]], sig=(kc == 7))
            k = op("act", lambda e, fc=fc, dst=dst, fn=fn: e.activation(
                out=dst[:, fc, :], in_=psb[bank][:, 0:512], func=fn),
                waits=[t, s3["uT_rd"][tb], s3.get("za_rd")])
            s3["ps_free"][bank] = k
            tu_toks.setdefault(Tt, {})[(part // 2, fc)] = k
            if part >= 2:
                tile_uz[Tt] = op("pool", lambda e, fc=fc: e.tensor_tensor(
                    out=uT[tb][:, fc, :], in0=uT[tb][:, fc, :], in1=zaT[tb][:, fc, :], op=ALU.mult),
                    waits=[k, tu_toks[Tt][(0, fc)]])
                s3["za_rd"] = tile_uz[Tt]

    def TU(Tt):
        for part in range(4):
            TUp(Tt, part)

    def SA1a(B):
        Tt, blk = B // 4, B % 4
        tb = Tt % 2
        pb = B % 3
        mt = tile_mt[Tt]
        t = None
        for kc in range(8):
            t = op("pe", lambda e, kc=kc: e.matmul(
                psb[2][:, 0:512], lhsT=hT3[tb][:, kc, blk * 128:(blk + 1) * 128], rhs=wu[:, kc, 512:1024],
                start=(kc == 0), stop=(kc == 7)), waits=mt + [cu, s3["ps_free"][2]], sig=(kc == 7))
        if blk == 3:
            s3["hT_rd"][tb] = t
        k_gv = op("act", lambda e: e.activation(out=gv[pb], in_=psb[2][:, 0:512], func=AF.Gelu_apprx_tanh),
                  waits=[t, s3["gv_rd"][pb]])
        s3["ps_free"][2] = k_gv
        k_st = op("dve", lambda e: e.bn_stats(out=stats[:, 0:6], in_=gv[pb]), waits=[k_gv, s3["stA_rd"]])
        k_ag = op("dve", lambda e: e.bn_aggr(out=stats[:, 6:8], in_=stats[:, 0:6]), waits=[k_st])
        k_e = op("pool", lambda e: e.tensor_scalar(out=stats[:, 8:9], in0=stats[:, 7:8], scalar1=LN_EPS,
                                                   scalar2=None, op0=ALU.add), waits=[k_ag])
        k_sd = op("pool", lambda e: e.tensor_tensor(out=stats[:, 9:10], in0=stats[:, 8:9], in1=mhalf[:, 0:1],
                                                    op=ALU.pow), waits=[k_e])
        sa1[B] = k_sd

    def SA1b(B):
        pb = B % 3
        k_sd = sa1[B]
        k_rs = k_sd
        k_n = op("dve", lambda e: e.scalar_tensor_tensor(
            out=vnrm, in0=gv[pb], scalar=stats[:, 6:7], in1=lnvg, op0=ALU.subtract, op1=ALU.mult),
            waits=[k_rs, t_lvg, s3["vnrm_rd"]])
        s3["gv_rd"][pb] = k_n
        k_vn = op("dve", lambda e: e.scalar_tensor_tensor(
            out=vn[pb], in0=vnrm, scalar=stats[:, 9:10], in1=lnvb, op0=ALU.mult, op1=ALU.add),
            waits=[k_n, t_lvb, s3["vn_rd"][pb]])
        s3["vnrm_rd"] = k_vn
        s3["stA_rd"] = k_vn
        blk_vn[B] = k_vn

    def SA2(B):
        Tt, blk = B // 4, B % 4
        tb = Tt % 2
        pb = B % 2
        p3 = B % 3
        k_vn = blk_vn[B]
        t = None
        for hp in range(4):
            for hl in range(2):
                t = op("pe", lambda e, hp=hp, hl=hl: e.matmul(
                    psb[3][hl * 64:(hl + 1) * 64, hp * 128:(hp + 1) * 128],
                    lhsT=vn[p3][:, (2 * hp + hl) * 64:(2 * hp + hl + 1) * 64], rhs=wsT[:, 2 * hp + hl, :],
                    start=True, stop=True), waits=[k_vn, k_ws, s3["ps_free"][3]],
                    sig=(hp == 3 and hl == 1))
        s3["vn_rd"][p3] = t
        k_mx = op("dve", lambda e: e.tensor_tensor(
            out=mx[:, :, :], in0=psb[3][:, 0:512].rearrange("p (a b) -> p a b", b=128), in1=bsT[:, :, :],
            op=ALU.add), waits=[t, k_bs, s3["mx_rd"]])
        s3["ps_free"][3] = k_mx
        k_oa = op("dve", lambda e: e.tensor_tensor(
            out=oaT[pb][:, :, :], in0=mx[:, :, :], in1=uT[tb][:, :, blk * 128:(blk + 1) * 128], op=ALU.mult),
            waits=[k_mx, tile_uz[Tt], s3["oa_rd"][pb]])
        s3["mx_rd"] = k_oa
        if blk == 3:
            s3["uT_rd"][tb] = k_oa
        blk_oa[B] = k_oa

    def SCa(B):
        pb = B % 2
        k_oa = blk_oa[B]
        ty = [None, None]
        for nh in range(2):
            bank = 4 + nh
            t = None
            for fc in range(8):
                lhs = oaT[pb][:, fc, :] if fc < 4 else obT[:, fc - 4, B * 128:(B + 1) * 128]
                t = op("pe", lambda e, fc=fc, nh=nh, lhs=lhs: e.matmul(
                    psb[bank][:, 0:512], lhsT=lhs, rhs=wo[:, fc, nh * 512:(nh + 1) * 512],
                    start=(fc == 0), stop=(fc == 7)), waits=[k_oa, co, s3["ps_free"][bank]], sig=(fc == 7))
            ty[nh] = t
        s3["oa_rd"][pb] = ty[1]
        t_xt = dma(f"xtok{pb}", xtok[pb], T["x_own"][B * 128:(B + 1) * 128, :], waits=[s3["x_rd"][pb]])
        k_r = None
        k_s = None
        for nh in range(2):
            k_r = op("dve", lambda e, nh=nh: e.scalar_tensor_tensor(
                out=rr[:, nh * 512:(nh + 1) * 512], in0=xtok[pb][:, nh * 512:(nh + 1) * 512], scalar=ALPHA,
                in1=psb[4 + nh][:, 0:512], op0=ALU.mult, op1=ALU.add), waits=[ty[nh], t_xt, s3["rr_rd"]])
            s3["ps_free"][4 + nh] = k_r
            k_s = op("dve", lambda e, nh=nh: e.bn_stats(out=stats[:, 12 + nh * 6:18 + nh * 6],
                                                        in_=rr[:, nh * 512:(nh + 1) * 512]),
                     waits=[k_r, s3["stC_rd"]])
        s3["x_rd"][pb] = k_r
        k_ag = op("dve", lambda e: e.bn_aggr(out=stats[:, 24:26], in_=stats[:, 12:24]), waits=[k_s])
        k_e = op("pool", lambda e: e.tensor_scalar(out=stats[:, 26:27], in0=stats[:, 25:26], scalar1=LN_EPS,
                                                   scalar2=None, op0=ALU.add), waits=[k_ag])
        k_sd = op("pool", lambda e: e.tensor_tensor(out=stats[:, 27:28], in0=stats[:, 26:27], in1=mhalf[:, 0:1],
                                                    op=ALU.pow), waits=[k_e])
        scs[B] = k_sd

    def SCb(B):
        pb = B % 2
        k_sd = scs[B]
        k_y = op("dve", lambda e: e.scalar_tensor_tensor(
            out=yo[pb], in0=rr, scalar=stats[:, 24:25], in1=lng, op0=ALU.subtract, op1=ALU.mult),
            waits=[k_sd, t_lng, s3["yo_rd"][pb]])
        s3["rr_rd"] = k_y
        s3["stC_rd"] = k_y
        k_y3 = op("dve", lambda e: e.scalar_tensor_tensor(
            out=yo[pb], in0=yo[pb], scalar=stats[:, 27:28], in1=lnb, op0=ALU.mult, op1=ALU.add),
            waits=[k_y, t_lnb])
        s3["stC_rd"] = k_y3
        t_o = dma(f"yout{pb}", T["out_own"][B * 128:(B + 1) * 128, :], yo[pb], waits=[k_y3])
        s3["yo_rd"][pb] = t_o
        out_toks.append(t_o)

    TL(0)
    TL(1)
    TU(0)
    SA1a(0)
    SA1b(0)
    SA1a(1)
    SA1b(1)
    for B in range(16):
        Tt, blk = B // 4, B % 4
        if blk == 0 and Tt + 2 < 4:
            TLd(Tt + 2)
        if Tt + 1 < 4:
            TUp(Tt + 1, blk)
        if B + 2 < 16:
            SA1a(B + 2)
        SA2(B)
        if B >= 1:
            SCa(B - 1)
        if B + 2 < 16:
            SA1b(B + 2)
        if blk == 2 and Tt + 2 < 4:
            TL(Tt + 2)
        if B >= 1:
            SCb(B - 1)
    SCa(15)
    SCb(15)
    P.wait("sp", out_toks[-2:])
    P.wait("act", out_toks[-2:])


_IN_SPECS = [
    ("xT_all", [DM, S + 1]), ("xT_own", [DM, NOWN]), ("x_own", [NOWN, DM]),
    ("w_in", [DM, 3584]), ("w_out", [DM, DM]), ("w_ada", [DM, 3072]), ("b_ada", [1, 3072]),
    ("cT", [128, 8]), ("ln_v_g", [1, 512]), ("ln_v_b", [1, 512]), ("ln_g", [1, DM]), ("ln_b", [1, DM]),
    ("wsT", [8, 128, 128]), ("bs_rev", [8, 128]), ("sel", [8, 4, 128]), ("dmat", [128, 512]),
    ("ident", [128, 128]),
]


def build_nc():
    nc = bass.Bass("TRN2", target_bir_lowering=False)
    T = {}
    for name, shape in _IN_SPECS:
        T[name] = nc.dram_tensor(name, shape, F32, kind="ExternalInput").ap()
    T["out_own"] = nc.dram_tensor("out_own", [NOWN, DM], F32, kind="ExternalOutput").ap()
    T["gate_scratch"] = nc.dram_tensor("gate_scratch", [1, DM], F32, kind="Internal").ap()
    with contextlib.ExitStack() as es:
        arena = es.enter_context(nc.sbuf_tensor("arena", [128, ARENA_WORDS], F32))
        psb = [es.enter_context(nc.psum_tensor(f"psb{i}", [128, 512], F32)) for i in range(8)]
        sems = {k: es.enter_context(nc.semaphore(f"s_{k}")) for k in ("pe", "act", "dve", "pool")}
        dsems = [es.enter_context(nc.semaphore(f"d_{i}")) for i in range(N_DSEM)]
        block = es.enter_context(nc.Block())
        arena_ap = arena[:, :]
        psb_ap = [p[:, :] for p in psb]

        @block.tensor
        def _(e):
            generate(Prog("pe", e, sems, dsems), nc, T, arena_ap, psb_ap)

        @block.scalar
        def _(e):
            generate(Prog("act", e, sems, dsems), nc, T, arena_ap, psb_ap)

        @block.vector
        def _(e):
            generate(Prog("dve", e, sems, dsems), nc, T, arena_ap, psb_ap)

        @block.gpsimd
        def _(e):
            generate(Prog("pool", e, sems, dsems), nc, T, arena_ap, psb_ap)

        @block.sync
        def _(e):
            generate(Prog("sp", e, sems, dsems), nc, T, arena_ap, psb_ap)
    return nc


def _own_idx(j):
    return (np.arange(16)[:, None] * 512 + 128 * j + np.arange(128)[None, :]).reshape(-1)


def kernel(x, c, w_ada, b_ada, w_in, ln_v_g, ln_v_b, w_spatial, b_spatial, w_out, ln_g, ln_b):
    f = lambda a: np.ascontiguousarray(np.asarray(a, dtype=np.float32))
    x = f(x); c = f(c)
    wsT = f(np.transpose(f(w_spatial)[0][:, ::-1, ::-1], (0, 2, 1)))
    bs_rev = f(f(b_spatial)[0][:, ::-1])
    sel = np.zeros((8, 4, 128), np.float32)
    for h in range(8):
        sel[h, h // 2, (h % 2) * 64:(h % 2) * 64 + 64] = 1.0
    ident = np.eye(128, dtype=np.float32)
    shared = {
        "w_in": f(w_in)[0], "w_out": f(w_out)[0], "w_ada": f(w_ada)[0], "b_ada": f(b_ada)[0][None, :],
        "ln_v_g": f(ln_v_g)[0][None, :], "ln_v_b": f(ln_v_b)[0][None, :],
        "ln_g": f(ln_g)[0][None, :], "ln_b": f(ln_b)[0][None, :],
        "wsT": wsT, "bs_rev": bs_rev, "sel": sel, "ident": ident,
    }
    in_maps = []
    for core in range(8):
        b, j = core // 4, core % 4
        xr = x[b, ::-1, :]
        xT_all = np.zeros((DM, S + 1), np.float32)
        xT_all[:, :S] = xr.T
        idx = _own_idx(j)
        x_own = f(xr[idx])
        dmat = np.zeros((128, 512), np.float32)
        dmat[np.arange(128), 128 * j + np.arange(128)] = 1.0
        m = dict(shared)
        m.update({"xT_all": xT_all, "xT_own": f(x_own.T), "x_own": x_own,
                  "cT": f(c[b].reshape(8, 128).T), "dmat": dmat})
        in_maps.append(m)
    nc = build_nc()
    res = run_bass_kernel_spmd(nc, in_maps, core_ids=list(range(8)))
    out = np.zeros((2, S, DM), np.float32)
    for core in range(8):
        b, j = core // 4, core % 4
        o_rev = res.results[core]["out_own"]
        out[b, S - 1 - _own_idx(j), :] = o_rev
    return out
```

```python
import contextlib
import numpy as np
import concourse.bass as bass
import concourse.mybir as mybir
from concourse.bass_utils import run_bass_kernel_spmd

F32 = mybir.dt.float32
BF16 = mybir.dt.bfloat16
ALU = mybir.AluOpType
AF = mybir.ActivationFunctionType

S = 8192
DM = 1024
NOWN = 2048
ALPHA = 2.0 ** 0.25
LN_EPS = 1e-5
ARENA_WORDS = 51200
N_DSEM = 40
STAGE = 99


class Prog:
    def __init__(self, cur, eobj, sems, dsems):
        self.cur = cur
        self.e = eobj
        self.sems = sems
        self.dsems = dsems
        self.cnt = {k: 0 for k in sems}
        self.waited = {}
        self.dmap = {}
        self.dcnt = {}

    def _sem(self, key):
        if key in self.sems:
            return self.sems[key]
        return self.dsems[self.dmap[key]]

    def wait(self, eng, toks):
        for t in toks:
            if t is None:
                continue
            key, val = t
            if val > self.waited.get((eng, key), 0):
                self.waited[(eng, key)] = val
                if eng == self.cur:
                    self.e.wait_ge(self._sem(key), val)

    def op(self, eng, fn, waits=(), sig=True):
        self.wait(eng, waits)
        tok = None
        if sig:
            self.cnt[eng] += 1
            tok = (eng, self.cnt[eng])
        if eng == self.cur:
            ins = fn(self.e)
            if sig:
                ins.then_inc(self.sems[eng], 1)
        return tok

    def dma(self, slot, out, in_, waits=(), q="sp"):
        self.wait(q, waits)
        if slot not in self.dmap:
            self.dmap[slot] = len(self.dmap)
            self.dcnt[slot] = 0
            assert len(self.dmap) <= len(self.dsems), "out of dma semaphores"
        self.dcnt[slot] += 16
        if q == self.cur:
            self.e.dma_start(out=out, in_=in_).then_inc(self.dsems[self.dmap[slot]], 16)
        return (slot, self.dcnt[slot])


class Arena:
    def __init__(self, ap):
        self.ap = ap
        self.off = 0
        self.hi = 0

    def mark(self):
        return self.off

    def reset(self, off):
        self.off = off

    def alloc(self, shape_free, dtype, parts=128):
        esz = 2 if dtype == BF16 else 4
        n = int(np.prod(shape_free)) * esz
        n = (n + 31) // 32 * 32
        off = self.off
        self.off += n
        self.hi = max(self.hi, self.off)
        assert self.off <= ARENA_WORDS * 4, f"arena overflow {self.off}"
        a = self.ap[0:parts, off // 4:(off + n) // 4]
        if dtype == BF16:
            a = a.bitcast(BF16)
        tot = int(np.prod(shape_free))
        a = a[:, 0:tot]
        if len(shape_free) == 2:
            a = a.rearrange("p (a b) -> p a b", b=shape_free[1])
        elif len(shape_free) == 3:
            a = a.rearrange("p (a b c) -> p a b c", b=shape_free[1], c=shape_free[2])
        return a


def generate(P, nc, T, arena_ap, psb):
    A = Arena(arena_ap)
    op = P.op
    dma = P.dma

    def psbf(b):
        return psb[b].bitcast(BF16)

    ident = A.alloc([128], BF16)
    zeros = A.alloc([512], F32)
    dmat = A.alloc([512], F32)
    sc1 = A.alloc([8], F32)
    sh = A.alloc([8], F32)
    epst = A.alloc([8], F32)
    ones_r = A.alloc([128], F32)
    mhalf = A.alloc([8], F32)
    obT = A.alloc([4, NOWN], BF16)
    G1 = A.alloc([1024], F32)
    base_mark = A.mark()
    wu = A.alloc([8, 1536], BF16)
    wo = A.alloc([8, 1024], BF16)
    wst3 = [A.alloc([1536], F32), A.alloc([1536], F32)]
    p3_head_end = A.mark()
    A.reset(base_mark)

    A.reset(base_mark)
    wh = A.alloc([8, 1024], BF16)
    wsx = A.alloc([2112], F32)
    wstage = [wsx[:, 0:1024], wsx[:, 1024:2048]]
    xs = [A.alloc([8, 258], F32), A.alloc([8, 258], F32),
          wsx[:, 0:2064].rearrange("p (a b) -> p a b", b=258)]
    hTt = [A.alloc([8, 258], BF16), A.alloc([8, 258], BF16)]
    vtmp = [A.alloc([2, 258], F32), A.alloc([2, 258], F32)]
    gTt = [A.alloc([2, 256], BF16), A.alloc([2, 256], BF16)]
    assert A.mark() >= p3_head_end, (A.mark(), p3_head_end)
    setup_mark = A.mark()
    kT = A.alloc([2, S], BF16)
    g = A.alloc([64, 256], BF16)
    qT = A.alloc([2, NOWN], BF16)
    vown = A.alloc([2, NOWN], F32)
    szb = A.alloc([2, NOWN], BF16)
    pbuf = [[A.alloc([512], F32) for _ in range(2)] for _ in range(2)]
    Ebuf = [[A.alloc([512], BF16) for _ in range(2)] for _ in range(3)]
    ETbuf = [[A.alloc([512], BF16) for _ in range(2)] for _ in range(2)]
    eptmp = [A.alloc([128], F32), A.alloc([128], F32)]

    xT_all = T["xT_all"].rearrange("(c p) t -> p c t", p=128)
    xT_own = T["xT_own"].rearrange("(c p) t -> p c t", p=128)

    st = {"mod_done": [None, None, None], "pe_done": [None, None], "gt_rd": [None, None],
          "vt_rd": [None, None], "bank_free": [None] * 8, "cast": None, "tile": 0}

    def load_half_weights(half, cast_waits=(), split_cast=False):
        cast_tok = None
        for kc in range(8):
            sb = kc % 2
            tw = None
            for gi, c0 in enumerate((2048, 2560, 1536, 3072)):
                tw = dma(f"wst{sb}", wstage[sb][:, gi * 256:(gi + 1) * 256],
                         T["w_in"][kc * 128:(kc + 1) * 128, c0 + half * 256:c0 + half * 256 + 256],
                         waits=[st.get(f"wcast{sb}")] + list(st["mod_done"][2] or []))
            if split_cast:
                cast_tok = op("act", lambda e, kc=kc, sb=sb: e.activation(
                    out=wh[:, kc, :], in_=wstage[sb], func=AF.Identity), waits=[tw] + list(cast_waits))
            else:
                cast_tok = op("pool", lambda e, kc=kc, sb=sb: e.tensor_copy(out=wh[:, kc, :], in_=wstage[sb]),
                              waits=[tw] + list(cast_waits))
            st[f"wcast{sb}"] = cast_tok
        st["cast"] = st["wcast1"]
        st["cast2"] = st["wcast0"]


    frame_end = A.mark()
    A.reset(setup_mark)
    ident32 = A.alloc([128], F32)
    cT = A.alloc([8], F32)
    bada = A.alloc([3072], F32, parts=1)
    modrow = A.alloc([3072], F32, parts=1)
    NWB = 6
    wst = [A.alloc([3072], F32) for _ in range(NWB)]

    t_id = dma("c_ident", ident32, T["ident"][:, :])
    t_dm = dma("c_dmat", dmat, T["dmat"][:, :])
    t_c = dma("c_cT", cT, T["cT"][:, :])
    t_ba = dma("c_bada", bada, T["b_ada"][0:1, :])
    k_z = op("dve", lambda e: e.memset(zeros, 0.0))
    k_one = op("dve", lambda e: e.memset(ones_r, 1.0))
    k_eps = op("dve", lambda e: e.memset(epst, LN_EPS))
    k_mh = op("dve", lambda e: e.memset(mhalf, -0.5))
    k_ident = op("dve", lambda e: e.tensor_copy(out=ident, in_=ident32), waits=[t_id])

    pe_last = [None] * NWB
    mm_tok = None
    for kc in range(8):
        wb = kc % NWB
        t_w = dma(f"wada{wb}", wst[wb], T["w_ada"][kc * 128:(kc + 1) * 128, :], waits=[pe_last[wb]])
        for ct in range(6):
            mm_tok = op("pe", lambda e, kc=kc, ct=ct, wb=wb: e.matmul(
                psb[ct][0:1, 0:512], lhsT=cT[:, kc:kc + 1], rhs=wst[wb][:, ct * 512:(ct + 1) * 512],
                start=(kc == 0), stop=(kc == 7)), waits=[t_w, t_c], sig=(ct == 5))
        pe_last[wb] = mm_tok
    pre_d = {}
    for n in range(2):
        pre_d[n] = (n, 257, dma(f"xs{n}", xs[n][:, :, 0:257], xT_all[:, :, n * 256:n * 256 + 257]))
    st["dtile"] = 2
    load_half_weights(0, split_cast=True)
    k_mod = None
    for ct in range(6):
        k_mod = op("dve", lambda e, ct=ct: e.tensor_tensor(
            out=modrow[:, ct * 512:(ct + 1) * 512], in0=psb[ct][0:1, 0:512],
            in1=bada[:, ct * 512:(ct + 1) * 512], op=ALU.add), waits=[mm_tok, t_ba])
    t_gsc = dma("gate_sc", T["gate_scratch"][0:1, :], modrow[:, 2048:3072], waits=[k_mod])
    tp = None
    for i in range(16):
        tp = op("pe", lambda e, i=i: e.matmul(
            psb[6][:, i:i + 1], lhsT=modrow[:, i * 128:(i + 1) * 128], rhs=ones_r[0:1, 0:1],
            start=True, stop=True), waits=[k_mod, k_one], sig=(i == 15))
    k_sh = op("dve", lambda e: e.tensor_copy(out=sh, in_=psb[6][:, 0:8]), waits=[tp])
    k_sc = op("dve", lambda e: e.tensor_scalar(out=sc1, in0=psb[6][:, 8:16], scalar1=1.0, scalar2=None,
                                               op0=ALU.add), waits=[tp])
    setup_done = [k_sc, k_sh, k_ident, k_z, k_eps, k_mh, t_dm, t_gsc, tp]
    assert A.mark() <= frame_end
    for eng in ("pe", "act", "dve", "pool"):
        P.wait(eng, setup_done)
    if STAGE == 0:
        return

    def modulate(buf, xb, ncols, t_x):
        toks = []
        for c in range(8):
            w = [t_x, st["pe_done"][buf]]
            if c < 5:
                toks.append(op("act", lambda e, c=c: e.activation(
                    out=hTt[buf][:, c, 0:ncols], in_=xs[xb][:, c, 0:ncols], func=AF.Identity,
                    scale=sc1[:, c:c + 1], bias=sh[:, c:c + 1]), waits=w))
            else:
                toks.append(op("pool", lambda e, c=c: e.tensor_scalar(
                    out=hTt[buf][:, c, 0:ncols], in0=xs[xb][:, c, 0:ncols],
                    scalar1=sc1[:, c:c + 1], scalar2=sh[:, c:c + 1], op0=ALU.mult, op1=ALU.add), waits=w))
        st["mod_done"][xb] = [toks[4], toks[7]]
        return [toks[4], toks[7]]

    def proj(buf, bank, col0, ncols, mtoks):
        tok = None
        for kc in range(8):
            tok = op("pe", lambda e, kc=kc: e.matmul(
                psb[bank][:, 0:ncols], lhsT=wh[:, kc, col0:col0 + 128], rhs=hTt[buf][:, kc, 0:ncols],
                start=(kc == 0), stop=(kc == 7)),
                waits=list(mtoks) + [st["bank_free"][bank], st["cast"], st.get("cast2")], sig=(kc == 7))
        return tok

    for half in range(2):
        tiles = [("kv", tt) for tt in range(32)] + [("own", ot) for ot in range(8)]
        tinfo = {}

        dinfo = {}

        def stage_D(n):
            kind, idx = tiles[n]
            xb = st.setdefault("dtile", 0) % 3
            st["dtile"] += 1
            if kind == "kv":
                ncols = 257
                src = xT_all[:, :, idx * 256:idx * 256 + 257]
            else:
                ncols = 256
                src = xT_own[:, :, idx * 256:(idx + 1) * 256]
            w = list(st["mod_done"][xb] or [])
            if xb == 2:
                w += [st.get("wcast0"), st.get("wcast1")]
            dinfo[n] = (xb, ncols, dma(f"xs{xb}", xs[xb][:, :, 0:ncols], src, waits=w))

        def stage_M(n):
            buf = st["tile"] % 2
            st["tile"] += 1
            xb, ncols, t_x = dinfo[n]
            mt = modulate(buf, xb, ncols, t_x)
            tinfo[n] = (buf, mt)

        def stage_C1(n):
            kind, idx = tiles[n]
            buf, mt = tinfo[n]
            pe_tok = None
            if kind == "kv":
                tt = idx
                for hp in range(2):
                    pe_tok = proj(buf, hp, hp * 128, 256, mt)
                    kk = op("dve", lambda e, hp=hp: e.tensor_copy(
                        out=kT[:, hp, tt * 256:(tt + 1) * 256], in_=psb[hp][:, 0:256]),
                        waits=[pe_tok, st.get("att_pe_done")])
                    st["bank_free"][hp] = kk
                vts = []
                for hp in range(2):
                    pe_tok = proj(buf, 2 + hp, 256 + hp * 128, 257, mt)
                    vk = op("dve", lambda e, hp=hp: e.tensor_copy(
                        out=vtmp[buf][:, hp, 0:257], in_=psb[2 + hp][:, 0:257]),
                        waits=[pe_tok, st["vt_rd"][buf]])
                    st["bank_free"][2 + hp] = vk
                    vts.append(vk)
                st["pe_done"][buf] = pe_tok
                if tt == 31:
                    vts.append(op("dve", lambda e: e.memset(vtmp[buf][:, :, 256:257], 0.0), waits=vts))
                gk = op("pool", lambda e: e.tensor_tensor(
                    out=gTt[buf][:, :, :], in0=vtmp[buf][:, :, 1:257], in1=vtmp[buf][:, :, 0:256],
                    op=ALU.subtract), waits=vts + [st["gt_rd"][buf]])
                st["vt_rd"][buf] = gk
                st["k_done"] = st["bank_free"][1]
                tinfo[n] = (buf, mt, gk)
            else:
                ot = idx
                for hp in range(2):
                    pe_tok = proj(buf, hp, 512 + hp * 128, 256, mt)
                    kk = op("dve", lambda e, hp=hp: e.tensor_copy(
                        out=qT[:, hp, ot * 256:(ot + 1) * 256], in_=psb[hp][:, 0:256]),
                        waits=[pe_tok, st.get("att_pe_done")])
                    st["bank_free"][hp] = kk
                for hp in range(2):
                    pe_tok = proj(buf, 2 + hp, 256 + hp * 128, 256, mt)
                    kk = op("dve", lambda e, hp=hp: e.tensor_copy(
                        out=vown[:, hp, ot * 256:(ot + 1) * 256], in_=psb[2 + hp][:, 0:256]),
                        waits=[pe_tok, st.get("att_ep_done")])
                    st["bank_free"][2 + hp] = kk
                for hp in range(2):
                    pe_tok = proj(buf, 6 + hp, 768 + hp * 128, 256, mt)
                    kk = op("act", lambda e, hp=hp: e.activation(
                        out=szb[:, hp, ot * 256:(ot + 1) * 256], in_=psb[6 + hp][:, 0:256], func=AF.Silu),
                        waits=[pe_tok, st.get("att_ep_done")])
                    st["bank_free"][6 + hp] = kk
                st["pe_done"][buf] = pe_tok
                st["q_done"] = [st["bank_free"][0], st["bank_free"][1], st["bank_free"][2],
                                st["bank_free"][3], st["bank_free"][6], st["bank_free"][7]]

        def stage_C2(n):
            kind, idx = tiles[n]
            if kind != "kv":
                return
            tt = idx
            buf, mt, gk = tinfo[n]
            tb = 4 + buf
            trt = None
            for blk in range(2):
                for hp in range(2):
                    trt = op("pe", lambda e, blk=blk, hp=hp: e.transpose(
                        psbf(tb)[:, (blk * 2 + hp) * 128:(blk * 2 + hp + 1) * 128],
                        gTt[buf][:, hp, blk * 128:(blk + 1) * 128], ident),
                        waits=[gk, st["bank_free"][tb]], sig=(blk == 1 and hp == 1))
            st["gt_rd"][buf] = trt
            gev = op("act", lambda e: e.activation(
                out=g[:, tt * 2:tt * 2 + 2, :], in_=psbf(tb)[:, 0:512].rearrange("p (a b) -> p a b", b=256),
                func=AF.Identity), waits=[trt, st.get("att_pe_done")])
            st["bank_free"][tb] = gev
            st["g_done"] = gev

        NTL = len(tiles)
        if half == 0:
            dinfo.update(pre_d)
        else:
            stage_D(0)
            stage_D(1)
        stage_M(0)
        for n in range(NTL):
            if n + 2 < NTL:
                stage_D(n + 2)
            if n + 1 < NTL:
                stage_M(n + 1)
            stage_C1(n)
            if n >= 1:
                stage_C2(n - 1)
        stage_C2(NTL - 1)

        if STAGE == 2:
            for eng in ("pe", "act", "dve", "pool", "sp"):
                P.wait(eng, list(st["q_done"]) + [st["pe_done"][0], st["pe_done"][1]])
            return
        if half == 0:
            load_half_weights(1, cast_waits=[st["pe_done"][0], st["pe_done"][1]])
        if half == 1:
            ph1_done = [st["pe_done"][0], st["pe_done"][1], st["gt_rd"][0], st["gt_rd"][1], st["vt_rd"][0],
                        st["vt_rd"][1], st["wcast0"], st["wcast1"], st["g_done"]] + \
                list(st["mod_done"][0] or []) + list(st["mod_done"][1] or []) + \
                list(st["mod_done"][2] or []) + list(st["q_done"])
            t_g1 = dma("p3_g1", G1, T["gate_scratch"][0:1, :].partition_broadcast(128), waits=[t_gsc])
            k_g1 = op("pool", lambda e: e.tensor_scalar(out=G1, in0=G1, scalar1=1.0, scalar2=None, op0=ALU.add),
                      waits=[t_g1])
            wtok = {}
            cu = None
            for kc in range(8):
                sb = kc % 2
                tw = dma(f"w3st{sb}", wst3[sb], T["w_in"][kc * 128:(kc + 1) * 128, 0:1536],
                         waits=ph1_done + [wtok.get(sb)])
                cu = op("pool", lambda e, kc=kc, sb=sb: e.tensor_copy(out=wu[:, kc, :], in_=wst3[sb]),
                        waits=[tw] + ph1_done)
                wtok[sb] = cu
            co = None
            for kc in range(8):
                sb = kc % 2
                tw = dma(f"w3st{sb}", wst3[sb][:, 0:1024], T["w_out"][kc * 128:(kc + 1) * 128, :],
                         waits=[wtok.get(sb)])
                co = op("pool", lambda e, kc=kc, sb=sb: e.tensor_tensor(
                    out=wo[:, kc, :], in0=wst3[sb][:, 0:1024], in1=G1, op=ALU.mult), waits=[tw, k_g1])
                wtok[sb] = co
        pus = [(m, hp, k) for m in range(16) for hp in range(2) for k in range(m, 16)]
        NPU = len(pus)
        ready = list(st["q_done"]) + [st["g_done"], st["k_done"], st["pe_done"][0], st["pe_done"][1],
                                      st["bank_free"][4], st["bank_free"][5]]
        tk = {}

        def qk(i):
            m, hp, k = pus[i]
            par = i % 2
            for hl in range(2):
                tk[("qk", i, hl)] = op("pe", lambda e, hl=hl: e.matmul(
                    psb[par * 2 + hl][:, 0:512],
                    lhsT=qT[hl * 64:(hl + 1) * 64, hp, m * 128:(m + 1) * 128],
                    rhs=kT[hl * 64:(hl + 1) * 64, hp, k * 512:(k + 1) * 512],
                    start=True, stop=True), waits=ready + [tk.get(("sig", i - 2, hl))])

        def sig(i):
            par = i % 2
            for hl in range(2):
                tk[("sig", i, hl)] = op("act", lambda e, hl=hl: e.activation(
                    out=pbuf[par][hl], in_=psb[par * 2 + hl][:, 0:512], func=AF.Sigmoid, scale=-0.125),
                    waits=[tk[("qk", i, hl)], tk.get(("scan", i - 2, hl))])

        def scan(i):
            m, hp, k = pus[i]
            par = i % 2
            for hl in range(2):
                first = (k == m)
                w = [tk[("sig", i, hl)], tk.get(("tr", i - 3, hl)), tk.get(("scan", i - 2, hl))]
                if not first:
                    w.append(tk[("scan", i - 1, hl)])
                    init = Ebuf[(i - 1) % 3][hl][:, 511:512]
                    d1 = zeros
                else:
                    init = 0.0
                    d1 = dmat
                tk[("scan", i, hl)] = op("dve", lambda e, hl=hl, init=init, d1=d1: e.tensor_tensor_scan(
                    out=Ebuf[i % 3][hl], data0=pbuf[par][hl], data1=d1, initial=init,
                    op0=ALU.mult, op1=ALU.add), waits=w)

        def tr(i):
            par = i % 2
            for hl in range(2):
                t = None
                for blk in range(4):
                    t = op("pe", lambda e, hl=hl, blk=blk: e.transpose(
                        psbf(4 + par)[:, hl * 512 + blk * 128:hl * 512 + (blk + 1) * 128],
                        Ebuf[i % 3][hl][:, blk * 128:(blk + 1) * 128], ident),
                        waits=[tk[("scan", i, hl)], tk.get(("ev", i - 2, 0)), tk.get(("ev", i - 2, 1))],
                        sig=(blk == 3))
                tk[("tr", i, hl)] = t

        def ev(i):
            par = i % 2
            for hl in range(2):
                tk[("ev", i, hl)] = op("act", lambda e, hl=hl: e.activation(
                    out=ETbuf[par][hl], in_=psbf(4 + par)[:, hl * 512:(hl + 1) * 512], func=AF.Identity),
                    waits=[tk[("tr", i, 0)], tk[("tr", i, 1)], tk.get(("wv", i - 2))])

        def wv(i):
            m, hp, k = pus[i]
            par = i % 2
            chain = m * 2 + hp
            ob = 6 + chain % 2
            while pend and pend[0][1] <= chain - 2:
                epilogue(pend.pop(0))
            t = None
            for blk in range(4):
                for hl in range(2):
                    t = op("pe", lambda e, hl=hl, blk=blk: e.matmul(
                        psb[ob][hl * 64:(hl + 1) * 64, 0:128],
                        lhsT=g[:, k * 4 + blk, hp * 128 + hl * 64:hp * 128 + (hl + 1) * 64],
                        rhs=ETbuf[par][hl][:, blk * 128:(blk + 1) * 128],
                        start=(k == m and blk == 0), stop=(k == 15 and blk == 3)),
                        waits=[tk[("ev", i, hl)], tk.get(("ep", chain - 2))], sig=(blk == 3 and hl == 1))
            tk[("wv", i)] = t
            if k == 15:
                pend.append((i, chain, m, hp, ob, t))

        pend = []

        def epilogue(ent):
            i, chain, m, hp, ob, t = ent
            e1 = op("dve", lambda e: e.tensor_tensor(
                out=eptmp[chain % 2], in0=psb[ob][:, 0:128], in1=vown[:, hp, m * 128:(m + 1) * 128],
                op=ALU.add), waits=[t, tk.get(("ep2", chain - 2))])
            tk[("ep", chain)] = e1
            tk[("ep2", chain)] = op("pool", lambda e: e.tensor_tensor(
                out=obT[:, half * 2 + hp, m * 128:(m + 1) * 128], in0=eptmp[chain % 2],
                in1=szb[:, hp, m * 128:(m + 1) * 128], op=ALU.mult), waits=[e1])
            st["att_ep_done"] = tk[("ep2", chain)]

        def flush_ep(upto):
            while pend and pend[0][0] <= upto:
                epilogue(pend.pop(0))

        qk(0)
        if NPU > 1:
            qk(1)
        sig(0)
        for i in range(NPU):
            if i + 1 < NPU:
                sig(i + 1)
            scan(i)
            flush_ep(i - 3)
            tr(i)
            if i + 2 < NPU:
                qk(i + 2)
            ev(i)
            if i >= 1:
                wv(i - 1)
        wv(NPU - 1)
        flush_ep(NPU)
        st["att_pe_done"] = tk[("wv", NPU - 1)]
        st["pe_all"] = tk[("wv", NPU - 1)]
        for b in range(8):
            st["bank_free"][b] = st["att_ep_done"] if b >= 6 else tk[("ev", NPU - 1, 1)]

        if STAGE == 3:
            for eng in ("pe", "act", "dve", "pool", "sp"):
                P.wait(eng, [st["att_ep_done"], st["att_pe_done"], tk[("ev", NPU - 1, 1)]])
            return
    fin = [st["att_ep_done"], st["att_pe_done"], tk[("ev", NPU - 1, 1)], st["wcast0"], st["wcast1"]]
    for eng in ("pe", "act", "dve", "pool", "sp"):
        P.wait(eng, fin)

    A.reset(p3_head_end)
    lng = A.alloc([1024], F32)
    lnb = A.alloc([1024], F32)
    lnvg = A.alloc([512], F32)
    lnvb = A.alloc([512], F32)
    bsT = A.alloc([4, 128], F32)
    wsT = A.alloc([8, 128], BF16)
    ws32 = A.alloc([8, 128], F32)
    sel = A.alloc([4, 128], F32, parts=8)
    bs8 = A.alloc([128], F32, parts=8)
    xs3_1 = A.alloc([8, 512], F32)
    xs3 = [xs3_1, xs3_1]
    hT3 = [A.alloc([8, 512], BF16), A.alloc([8, 512], BF16)]
    uT = [A.alloc([4, 512], F32), A.alloc([4, 512], F32)]
    zaT_1 = A.alloc([4, 512], F32)
    zaT = [zaT_1, zaT_1]
    gv = [A.alloc([512], F32) for _ in range(3)]
    vnrm = A.alloc([512], F32)
    vn = [A.alloc([512], BF16) for _ in range(3)]
    mx = A.alloc([4, 128], F32)
    oaT = [A.alloc([4, 128], BF16), A.alloc([4, 128], BF16)]
    xtok = [A.alloc([1024], F32), A.alloc([1024], F32)]
    rr = A.alloc([1024], F32)
    yo = [A.alloc([1024], F32), A.alloc([1024], F32)]
    stats = A.alloc([32], F32)

    pre_tl0 = dma("xs3", xs3[0], xT_own[:, :, 0:512])
    t_lng = dma("p3_lng", lng, T["ln_g"][0:1, :].partition_broadcast(128))
    t_lnb = dma("p3_lnb", lnb, T["ln_b"][0:1, :].partition_broadcast(128))
    t_lvg = dma("p3_lvg", lnvg, T["ln_v_g"][0:1, :].partition_broadcast(128))
    t_lvb = dma("p3_lvb", lnvb, T["ln_v_b"][0:1, :].partition_broadcast(128))
    t_ws = dma("p3_ws", ws32, T["wsT"].rearrange("h j i -> j h i"))
    t_sel = dma("p3_sel", sel, T["sel"])
    t_bs = dma("p3_bs", bs8, T["bs_rev"][:, :])
    k_ws = op("dve", lambda e: e.tensor_copy(out=wsT, in_=ws32), waits=[t_ws])
    k_ws = op("dve", lambda e: e.memset(wsT[0:64, :, 64:128], 0.0), waits=[k_ws])
    tb_ = None
    for hp in range(4):
        tb_ = op("pe", lambda e, hp=hp: e.matmul(
            psb[0][:, hp * 128:(hp + 1) * 128], lhsT=sel[:, hp, :], rhs=bs8[:, :], start=True, stop=True),
            waits=[t_sel, t_bs], sig=(hp == 3))
    k_bs = op("dve", lambda e: e.tensor_copy(out=bsT, in_=psb[0][:, 0:512].rearrange("p (a b) -> p a b", b=128)),
              waits=[tb_])
    s3 = {"hT_rd": [None, None], "xs_rd": [None, None], "ps_free": [k_bs] + [None] * 7,
          "uT_rd": [None, None], "gv_rd": [None, None, None], "vn_rd": [None, None, None], "oa_rd": [None, None],
          "x_rd": [None, None], "yo_rd": [None, None], "mx_rd": None, "rr_rd": None, "vnrm_rd": None,
          "stA_rd": None, "stC_rd": None}
    out_toks = []
    tile_mt = {}
    tile_uz = {}
    blk_oa = {}
    blk_vn = {}
    sa1 = {}
    scs = {}

    tl_dma = {}

    def TLd(Tt):
        tb = Tt % 2
        if Tt == 0:
            tl_dma[0] = pre_tl0
            return
        tl_dma[Tt] = dma("xs3", xs3[tb], xT_own[:, :, Tt * 512:(Tt + 1) * 512], waits=s3["xs_rd"][0] or [])

    def TL(Tt):
        tb = Tt % 2
        if Tt not in tl_dma:
            TLd(Tt)
        t_x = tl_dma[Tt]
        mtoks = []
        for c in range(8):
            w = [t_x, s3["hT_rd"][tb]]
            mtoks.append(op("act", lambda e, c=c: e.activation(
                out=hT3[tb][:, c, :], in_=xs3[tb][:, c, :], func=AF.Identity,
                scale=sc1[:, c:c + 1], bias=sh[:, c:c + 1]), waits=w))
        s3["xs_rd"][0] = [mtoks[4], mtoks[7]]
        tile_mt[Tt] = [mtoks[4], mtoks[7]]

    tu_toks = {}

    def TUp(Tt, part):
        tb = Tt % 2
        mt = tile_mt[Tt]
        c0, dst, fn = ((0, uT[tb], AF.Gelu_apprx_tanh), (1024, zaT[tb], AF.Silu))[part // 2]
        for fc in ((part % 2) * 2, (part % 2) * 2 + 1):
            bank = fc % 2
            t = None
            for kc in range(8):
                t = op("pe", lambda e, kc=kc, fc=fc, c0=c0: e.matmul(
                    psb[bank][:, 0:512], lhsT=wu[:, kc, c0 + fc * 128:c0 + (fc + 1) * 128],
                    rhs=hT3[tb][:, kc, :], start=(kc == 0), stop=(kc == 7)),
                    waits=mt + [cu, s3["ps_free"][bank]], sig=(kc == 7))
            k = op("act", lambda e, fc=fc, dst=dst, fn=fn: e.activation(
                out=dst[:, fc, :], in_=psb[bank][:, 0:512], func=fn),
                waits=[t, s3["uT_rd"][tb], s3.get("za_rd")])
            s3["ps_free"][bank] = k
            tu_toks.setdefault(Tt, {})[(part // 2, fc)] = k
            if part >= 2:
                tile_uz[Tt] = op("pool", lambda e, fc=fc: e.tensor_tensor(
                    out=uT[tb][:, fc, :], in0=uT[tb][:, fc, :], in1=zaT[tb][:, fc, :], op=ALU.mult),
                    waits=[k, tu_toks[Tt][(0, fc)]])
                s3["za_rd"] = tile_uz[Tt]

    def TU(Tt):
        for part in range(4):
            TUp(Tt, part)

    def SA1a(B):
        Tt, blk = B // 4, B % 4
        tb = Tt % 2
        pb = B % 3
        mt = tile_mt[Tt]
        t = None
        for kc in range(8):
            t = op("pe", lambda e, kc=kc: e.matmul(
                psb[2][:, 0:512], lhsT=hT3[tb][:, kc, blk * 128:(blk + 1) * 128], rhs=wu[:, kc, 512:1024],
                start=(kc == 0), stop=(kc == 7)), waits=mt + [cu, s3["ps_free"][2]], sig=(kc == 7))
        if blk == 3:
            s3["hT_rd"][tb] = t
        k_gv = op("act", lambda e: e.activation(out=gv[pb], in_=psb[2][:, 0:512], func=AF.Gelu_apprx_tanh),
                  waits=[t, s3["gv_rd"][pb]])
        s3["ps_free"][2] = k_gv
        k_st = op("dve", lambda e: e.bn_stats(out=stats[:, 0:6], in_=gv[pb]), waits=[k_gv, s3["stA_rd"]])
        k_ag = op("dve", lambda e: e.bn_aggr(out=stats[:, 6:8], in_=stats[:, 0:6]), waits=[k_st])
        k_e = op("pool", lambda e: e.tensor_scalar(out=stats[:, 8:9], in0=stats[:, 7:8], scalar1=LN_EPS,
                                                   scalar2=None, op0=ALU.add), waits=[k_ag])
        k_sd = op("pool", lambda e: e.tensor_tensor(out=stats[:, 9:10], in0=stats[:, 8:9], in1=mhalf[:, 0:1],
                                                    op=ALU.pow), waits=[k_e])
        sa1[B] = k_sd

    def SA1b(B):
        pb = B % 3
        k_sd = sa1[B]
        k_rs = k_sd
        k_n = op("dve", lambda e: e.scalar_tensor_tensor(
            out=vnrm, in0=gv[pb], scalar=stats[:, 6:7], in1=lnvg, op0=ALU.subtract, op1=ALU.mult),
            waits=[k_rs, t_lvg, s3["vnrm_rd"]])
        s3["gv_rd"][pb] = k_n
        k_vn = op("dve", lambda e: e.scalar_tensor_tensor(
            out=vn[pb], in0=vnrm, scalar=stats[:, 9:10], in1=lnvb, op0=ALU.mult, op1=ALU.add),
            waits=[k_n, t_lvb, s3["vn_rd"][pb]])
        s3["vnrm_rd"] = k_vn
        s3["stA_rd"] = k_vn
        blk_vn[B] = k_vn

    def SA2(B):
        Tt, blk = B // 4, B % 4
        tb = Tt % 2
        pb = B % 2
        p3 = B % 3
        k_vn = blk_vn[B]
        t = None
        for hp in range(4):
            for hl in range(2):
                t = op("pe", lambda e, hp=hp, hl=hl: e.matmul(
                    psb[3][hl * 64:(hl + 1) * 64, hp * 128:(hp + 1) * 128],
                    lhsT=vn[p3][:, (2 * hp + hl) * 64:(2 * hp + hl + 1) * 64], rhs=wsT[:, 2 * hp + hl, :],
                    start=True, stop=True), waits=[k_vn, k_ws, s3["ps_free"][3]],
                    sig=(hp == 3 and hl == 1))
        s3["vn_rd"][p3] = t
        k_mx = op("dve", lambda e: e.tensor_tensor(
            out=mx[:, :, :], in0=psb[3][:, 0:512].rearrange("p (a b) -> p a b", b=128), in1=bsT[:, :, :],
            op=ALU.add), waits=[t, k_bs, s3["mx_rd"]])
        s3["ps_free"][3] = k_mx
        k_oa = op("dve", lambda e: e.tensor_tensor(
            out=oaT[pb][:, :, :], in0=mx[:, :, :], in1=uT[tb][:, :, blk * 128:(blk + 1) * 128], op=ALU.mult),
            waits=[k_mx, tile_uz[Tt], s3["oa_rd"][pb]])
        s3["mx_rd"] = k_oa
        if blk == 3:
            s3["uT_rd"][tb] = k_oa
        blk_oa[B] = k_oa

    def SCa(B):
        pb = B % 2
        k_oa = blk_oa[B]
        ty = [None, None]
        for nh in range(2):
            bank = 4 + nh
            t = None
            for fc in range(8):
                lhs = oaT[pb][:, fc, :] if fc < 4 else obT[:, fc - 4, B * 128:(B + 1) * 128]
                t = op("pe", lambda e, fc=fc, nh=nh, lhs=lhs: e.matmul(
                    psb[bank][:, 0:512], lhsT=lhs, rhs=wo[:, fc, nh * 512:(nh + 1) * 512],
                    start=(fc == 0), stop=(fc == 7)), waits=[k_oa, co, s3["ps_free"][bank]], sig=(fc == 7))
            ty[nh] = t
        s3["oa_rd"][pb] = ty[1]
        t_xt = dma(f"xtok{pb}", xtok[pb], T["x_own"][B * 128:(B + 1) * 128, :], waits=[s3["x_rd"][pb]])
        k_r = None
        k_s = None
        for nh in range(2):
            k_r = op("dve", lambda e, nh=nh: e.scalar_tensor_tensor(
                out=rr[:, nh * 512:(nh + 1) * 512], in0=xtok[pb][:, nh * 512:(nh + 1) * 512], scalar=ALPHA,
                in1=psb[4 + nh][:, 0:512], op0=ALU.mult, op1=ALU.add), waits=[ty[nh], t_xt, s3["rr_rd"]])
            s3["ps_free"][4 + nh] = k_r
            k_s = op("dve", lambda e, nh=nh: e.bn_stats(out=stats[:, 12 + nh * 6:18 + nh * 6],
                                                        in_=rr[:, nh * 512:(nh + 1) * 512]),
                     waits=[k_r, s3["stC_rd"]])
        s3["x_rd"][pb] = k_r
        k_ag = op("dve", lambda e: e.bn_aggr(out=stats[:, 24:26], in_=stats[:, 12:24]), waits=[k_s])
        k_e = op("pool", lambda e: e.tensor_scalar(out=stats[:, 26:27], in0=stats[:, 25:26], scalar1=LN_EPS,
                                                   scalar2=None, op0=ALU.add), waits=[k_ag])
        k_sd = op("pool", lambda e: e.tensor_tensor(out=stats[:, 27:28], in0=stats[:, 26:27], in1=mhalf[:, 0:1],
                                                    op=ALU.pow), waits=[k_e])
        scs[B] = k_sd

    def SCb(B):
        pb = B % 2
        k_sd = scs[B]
        k_y = op("dve", lambda e: e.scalar_tensor_tensor(
            out=yo[pb], in0=rr, scalar=stats[:, 24:25], in1=lng, op0=ALU.subtract, op1=ALU.mult),
            waits=[k_sd, t_lng, s3["yo_rd"][pb]])
        s3["rr_rd"] = k_y
        s3["stC_rd"] = k_y
        k_y3 = op("dve", lambda e: e.scalar_tensor_tensor(
            out=yo[pb], in0=yo[pb], scalar=stats[:, 27:28], in1=lnb, op0=ALU.mult, op1=ALU.add),
            waits=[k_y, t_lnb])
        s3["stC_rd"] = k_y3
        t_o = dma(f"yout{pb}", T["out_own"][B * 128:(B + 1) * 128, :], yo[pb], waits=[k_y3])
        s3["yo_rd"][pb] = t_o
        out_toks.append(t_o)

    TL(0)
    TL(1)
    TU(0)
    SA1a(0)
    SA1b(0)
    SA1a(1)
    SA1b(1)
    for B in range(16):
        Tt, blk = B // 4, B % 4
        if blk == 0 and Tt + 2 < 4:
            TLd(Tt + 2)
        if Tt + 1 < 4:
            TUp(Tt + 1, blk)
        if B + 2 < 16:
            SA1a(B + 2)
        SA2(B)
        if B >= 1:
            SCa(B - 1)
        if B + 2 < 16:
            SA1b(B + 2)
        if blk == 2 and Tt + 2 < 4:
            TL(Tt + 2)
        if B >= 1:
            SCb(B - 1)
    SCa(15)
    SCb(15)
    P.wait("sp", out_toks[-2:])
    P.wait("act", out_toks[-2:])


_IN_SPECS = [
    ("xT_all", [DM, S + 1]), ("xT_own", [DM, NOWN]), ("x_own", [NOWN, DM]),
    ("w_in", [DM, 3584]), ("w_out", [DM, DM]), ("w_ada", [DM, 3072]), ("b_ada", [1, 3072]),
    ("cT", [128, 8]), ("ln_v_g", [1, 512]), ("ln_v_b", [1, 512]), ("ln_g", [1, DM]), ("ln_b", [1, DM]),
    ("wsT", [8, 128, 128]), ("bs_rev", [8, 128]), ("sel", [8, 4, 128]), ("dmat", [128, 512]),
    ("ident", [128, 128]),
]


def build_nc():
    nc = bass.Bass("TRN2", target_bir_lowering=False)
    T = {}
    for name, shape in _IN_SPECS:
        T[name] = nc.dram_tensor(name, shape, F32, kind="ExternalInput").ap()
    T["out_own"] = nc.dram_tensor("out_own", [NOWN, DM], F32, kind="ExternalOutput").ap()
    T["gate_scratch"] = nc.dram_tensor("gate_scratch", [1, DM], F32, kind="Internal").ap()
    with contextlib.ExitStack() as es:
        arena = es.enter_context(nc.sbuf_tensor("arena", [128, ARENA_WORDS], F32))
        psb = [es.enter_context(nc.psum_tensor(f"psb{i}", [128, 512], F32)) for i in range(8)]
        sems = {k: es.enter_context(nc.semaphore(f"s_{k}")) for k in ("pe", "act", "dve", "pool")}
        dsems = [es.enter_context(nc.semaphore(f"d_{i}")) for i in range(N_DSEM)]
        block = es.enter_context(nc.Block())
        arena_ap = arena[:, :]
        psb_ap = [p[:, :] for p in psb]

        @block.tensor
        def _(e):
            generate(Prog("pe", e, sems, dsems), nc, T, arena_ap, psb_ap)

        @block.scalar
        def _(e):
            generate(Prog("act", e, sems, dsems), nc, T, arena_ap, psb_ap)

        @block.vector
        def _(e):
            generate(Prog("dve", e, sems, dsems), nc, T, arena_ap, psb_ap)

        @block.gpsimd
        def _(e):
            generate(Prog("pool", e, sems, dsems), nc, T, arena_ap, psb_ap)

        @block.sync
        def _(e):
            generate(Prog("sp", e, sems, dsems), nc, T, arena_ap, psb_ap)
    return nc


def _own_idx(j):
    return (np.arange(16)[:, None] * 512 + 128 * j + np.arange(128)[None, :]).reshape(-1)


def kernel(x, c, w_ada, b_ada, w_in, ln_v_g, ln_v_b, w_spatial, b_spatial, w_out, ln_g, ln_b):
    f = lambda a: np.ascontiguousarray(np.asarray(a, dtype=np.float32))
    x = f(x); c = f(c)
    wsT = f(np.transpose(f(w_spatial)[0][:, ::-1, ::-1], (0, 2, 1)))
    bs_rev = f(f(b_spatial)[0][:, ::-1])
    sel = np.zeros((8, 4, 128), np.float32)
    for h in range(8):
        sel[h, h // 2, (h % 2) * 64:(h % 2) * 64 + 64] = 1.0
    ident = np.eye(128, dtype=np.float32)
    shared = {
        "w_in": f(w_in)[0], "w_out": f(w_out)[0], "w_ada": f(w_ada)[0], "b_ada": f(b_ada)[0][None, :],
        "ln_v_g": f(ln_v_g)[0][None, :], "ln_v_b": f(ln_v_b)[0][None, :],
        "ln_g": f(ln_g)[0][None, :], "ln_b": f(ln_b)[0][None, :],
        "wsT": wsT, "bs_rev": bs_rev, "sel": sel, "ident": ident,
    }
    in_maps = []
    for core in range(8):
        b, j = core // 4, core % 4
        xr = x[b, ::-1, :]
        xT_all = np.zeros((DM, S + 1), np.float32)
        xT_all[:, :S] = xr.T
        idx = _own_idx(j)
        x_own = f(xr[idx])
        dmat = np.zeros((128, 512), np.float32)
        dmat[np.arange(128), 128 * j + np.arange(128)] = 1.0
        m = dict(shared)
        m.update({"xT_all": xT_all, "xT_own": f(x_own.T), "x_own": x_own,
                  "cT": f(c[b].reshape(8, 128).T), "dmat": dmat})
        in_maps.append(m)
    nc = build_nc()
    res = run_bass_kernel_spmd(nc, in_maps, core_ids=list(range(8)))
    out = np.zeros((2, S, DM), np.float32)
    for core in range(8):
        b, j = core // 4, core % 4
        o_rev = res.results[core]["out_own"]
        out[b, S - 1 - _own_idx(j), :] = o_rev
    return out
```

```python
import contextlib
import numpy as np
import concourse.bass as bass
import concourse.mybir as mybir
from concourse.bass_utils import run_bass_kernel_spmd

F32 = mybir.dt.float32
BF16 = mybir.dt.bfloat16
ALU = mybir.AluOpType
AF = mybir.ActivationFunctionType

S = 8192
DM = 1024
NOWN = 2048
ALPHA = 2.0 ** 0.25
LN_EPS = 1e-5
ARENA_WORDS = 51200
N_DSEM = 40
STAGE = 99


class Prog:
    def __init__(self, cur, eobj, sems, dsems):
        self.cur = cur
        self.e = eobj
        self.sems = sems
        self.dsems = dsems
        self.cnt = {k: 0 for k in sems}
        self.waited = {}
        self.dmap = {}
        self.dcnt = {}

    def _sem(self, key):
        if key in self.sems:
            return self.sems[key]
        return self.dsems[self.dmap[key]]

    def wait(self, eng, toks):
        for t in toks:
            if t is None:
                continue
            key, val = t
            if val > self.waited.get((eng, key), 0):
                self.waited[(eng, key)] = val
                if eng == self.cur:
                    self.e.wait_ge(self._sem(key), val)

    def op(self, eng, fn, waits=(), sig=True):
        self.wait(eng, waits)
        tok = None
        if sig:
            self.cnt[eng] += 1
            tok = (eng, self.cnt[eng])
        if eng == self.cur:
            ins = fn(self.e)
            if sig:
                ins.then_inc(self.sems[eng], 1)
        return tok

    def dma(self, slot, out, in_, waits=(), q="sp"):
        self.wait(q, waits)
        if slot not in self.dmap:
            self.dmap[slot] = len(self.dmap)
            self.dcnt[slot] = 0
            assert len(self.dmap) <= len(self.dsems), "out of dma semaphores"
        self.dcnt[slot] += 16
        if q == self.cur:
            self.e.dma_start(out=out, in_=in_).then_inc(self.dsems[self.dmap[slot]], 16)
        return (slot, self.dcnt[slot])


class Arena:
    def __init__(self, ap):
        self.ap = ap
        self.off = 0
        self.hi = 0

    def mark(self):
        return self.off

    def reset(self, off):
        self.off = off

    def alloc(self, shape_free, dtype, parts=128):
        esz = 2 if dtype == BF16 else 4
        n = int(np.prod(shape_free)) * esz
        n = (n + 31) // 32 * 32
        off = self.off
        self.off += n
        self.hi = max(self.hi, self.off)
        assert self.off <= ARENA_WORDS * 4, f"arena overflow {self.off}"
        a = self.ap[0:parts, off // 4:(off + n) // 4]
        if dtype == BF16:
            a = a.bitcast(BF16)
        tot = int(np.prod(shape_free))
        a = a[:, 0:tot]
        if len(shape_free) == 2:
            a = a.rearrange("p (a b) -> p a b", b=shape_free[1])
        elif len(shape_free) == 3:
            a = a.rearrange("p (a b c) -> p a b c", b=shape_free[1], c=shape_free[2])
        return a


def generate(P, nc, T, arena_ap, psb):
    A = Arena(arena_ap)
    op = P.op
    dma = P.dma

    def psbf(b):
        return psb[b].bitcast(BF16)

    ident = A.alloc([128], BF16)
    zeros = A.alloc([512], F32)
    dmat = A.alloc([512], F32)
    sc1 = A.alloc([8], F32)
    sh = A.alloc([8], F32)
    epst = A.alloc([8], F32)
    ones_r = A.alloc([128], F32)
    mhalf = A.alloc([8], F32)
    obT = A.alloc([4, NOWN], BF16)
    G1 = A.alloc([1024], F32)
    base_mark = A.mark()
    wu = A.alloc([8, 1536], BF16)
    wo = A.alloc([8, 1024], BF16)
    wst3 = [A.alloc([1536], F32), A.alloc([1536], F32)]
    p3_head_end = A.mark()
    A.reset(base_mark)

    A.reset(base_mark)
    wh = A.alloc([8, 1024], BF16)
    wsx = A.alloc([2112], F32)
    wstage = [wsx[:, 0:1024], wsx[:, 1024:2048]]
    xs = [A.alloc([8, 258], F32), A.alloc([8, 258], F32),
          wsx[:, 0:2064].rearrange("p (a b) -> p a b", b=258)]
    hTt = [A.alloc([8, 258], BF16), A.alloc([8, 258], BF16)]
    vtmp = [A.alloc([2, 258], F32), A.alloc([2, 258], F32)]
    gTt = [A.alloc([2, 256], BF16), A.alloc([2, 256], BF16)]
    assert A.mark() >= p3_head_end, (A.mark(), p3_head_end)
    setup_mark = A.mark()
    kT = A.alloc([2, S], BF16)
    g = A.alloc([64, 256], BF16)
    qT = A.alloc([2, NOWN], BF16)
    vown = A.alloc([2, NOWN], F32)
    szb = A.alloc([2, NOWN], BF16)
    pbuf = [[A.alloc([512], F32) for _ in range(2)] for _ in range(2)]
    Ebuf = [[A.alloc([512], BF16) for _ in range(2)] for _ in range(3)]
    ETbuf = [[A.alloc([512], BF16) for _ in range(2)] for _ in range(2)]
    eptmp = [A.alloc([128], F32), A.alloc([128], F32)]

    xT_all = T["xT_all"].rearrange("(c p) t -> p c t", p=128)
    xT_own = T["xT_own"].rearrange("(c p) t -> p c t", p=128)

    st = {"mod_done": [None, None, None], "pe_done": [None, None], "gt_rd": [None, None],
          "vt_rd": [None, None], "bank_free": [None] * 8, "cast": None, "tile": 0}

    bg = []

    def half_weight_tasks(half, cast_waits=()):
        tws = {}

        def issue(kc):
            sb = kc % 2
            tw = None
            for gi, c0 in enumerate((2048, 2560, 1536, 3072)):
                tw = dma(f"wst{sb}", wstage[sb][:, gi * 256:(gi + 1) * 256],
                         T["w_in"][kc * 128:(kc + 1) * 128, c0 + half * 256:c0 + half * 256 + 256],
                         waits=[st.get(f"wcast{sb}")] + list(st["mod_done"][2] or []))
            tws[kc] = tw

        def task(kc):
            sb = kc % 2
            st[f"wcast{sb}"] = op("act", lambda e: e.activation(
                out=wh[:, kc, :], in_=wstage[sb], func=AF.Identity), waits=[tws[kc]] + list(cast_waits))
            if kc + 2 < 8:
                issue(kc + 2)
            if kc == 7:
                st["cast"] = st["wcast1"]
                st["cast2"] = st["wcast0"]

        issue(0)
        issue(1)
        return [lambda kc=kc: task(kc) for kc in range(8)]

    frame_end = A.mark()
    A.reset(setup_mark)
    ident32 = A.alloc([128], F32)
    cT = A.alloc([8], F32)
    bada = A.alloc([3072], F32, parts=1)
    modrow = A.alloc([3072], F32, parts=1)
    NWB = 6
    wst = [A.alloc([3072], F32) for _ in range(NWB)]

    t_id = dma("c_ident", ident32, T["ident"][:, :])
    t_dm = dma("c_dmat", dmat, T["dmat"][:, :])
    t_c = dma("c_cT", cT, T["cT"][:, :])
    t_ba = dma("c_bada", bada, T["b_ada"][0:1, :])
    k_z = op("dve", lambda e: e.memset(zeros, 0.0))
    k_one = op("dve", lambda e: e.memset(ones_r, 1.0))
    k_eps = op("dve", lambda e: e.memset(epst, LN_EPS))
    k_mh = op("dve", lambda e: e.memset(mhalf, -0.5))
    k_ident = op("dve", lambda e: e.tensor_copy(out=ident, in_=ident32), waits=[t_id])

    pe_last = [None] * NWB
    mm_tok = None
    for kc in range(8):
        wb = kc % NWB
        t_w = dma(f"wada{wb}", wst[wb], T["w_ada"][kc * 128:(kc + 1) * 128, :], waits=[pe_last[wb]])
        for ct in range(6):
            mm_tok = op("pe", lambda e, kc=kc, ct=ct, wb=wb: e.matmul(
                psb[ct][0:1, 0:512], lhsT=cT[:, kc:kc + 1], rhs=wst[wb][:, ct * 512:(ct + 1) * 512],
                start=(kc == 0), stop=(kc == 7)), waits=[t_w, t_c], sig=(ct == 5))
        pe_last[wb] = mm_tok
    pre_d = {}
    for n in range(2):
        pre_d[n] = (n, 257, dma(f"xs{n}", xs[n][:, :, 0:257], xT_all[:, :, n * 256:n * 256 + 257]))
    st["dtile"] = 2
    for _t in half_weight_tasks(0):
        _t()
    k_mod = None
    for ct in range(6):
        k_mod = op("dve", lambda e, ct=ct: e.tensor_tensor(
            out=modrow[:, ct * 512:(ct + 1) * 512], in0=psb[ct][0:1, 0:512],
            in1=bada[:, ct * 512:(ct + 1) * 512], op=ALU.add), waits=[mm_tok, t_ba])
    t_gsc = dma("gate_sc", T["gate_scratch"][0:1, :], modrow[:, 2048:3072], waits=[k_mod])
    tp = None
    for i in range(16):
        tp = op("pe", lambda e, i=i: e.matmul(
            psb[6][:, i:i + 1], lhsT=modrow[:, i * 128:(i + 1) * 128], rhs=ones_r[0:1, 0:1],
            start=True, stop=True), waits=[k_mod, k_one], sig=(i == 15))
    k_sh = op("dve", lambda e: e.tensor_copy(out=sh, in_=psb[6][:, 0:8]), waits=[tp])
    k_sc = op("dve", lambda e: e.tensor_scalar(out=sc1, in0=psb[6][:, 8:16], scalar1=1.0, scalar2=None,
                                               op0=ALU.add), waits=[tp])
    setup_done = [k_sc, k_sh, k_ident, k_z, k_eps, k_mh, t_dm, t_gsc, tp]
    assert A.mark() <= frame_end
    for eng in ("pe", "act", "dve", "pool"):
        P.wait(eng, setup_done)
    if STAGE == 0:
        return

    def modulate(buf, xb, ncols, t_x):
        toks = []
        for c in range(8):
            w = [t_x, st["pe_done"][buf]]
            if c < 5:
                toks.append(op("act", lambda e, c=c: e.activation(
                    out=hTt[buf][:, c, 0:ncols], in_=xs[xb][:, c, 0:ncols], func=AF.Identity,
                    scale=sc1[:, c:c + 1], bias=sh[:, c:c + 1]), waits=w))
            else:
                toks.append(op("pool", lambda e, c=c: e.tensor_scalar(
                    out=hTt[buf][:, c, 0:ncols], in0=xs[xb][:, c, 0:ncols],
                    scalar1=sc1[:, c:c + 1], scalar2=sh[:, c:c + 1], op0=ALU.mult, op1=ALU.add), waits=w))
        st["mod_done"][xb] = [toks[4], toks[7]]
        return [toks[4], toks[7]]

    def proj(buf, bank, col0, ncols, mtoks):
        tok = None
        for kc in range(8):
            tok = op("pe", lambda e, kc=kc: e.matmul(
                psb[bank][:, 0:ncols], lhsT=wh[:, kc, col0:col0 + 128], rhs=hTt[buf][:, kc, 0:ncols],
                start=(kc == 0), stop=(kc == 7)),
                waits=list(mtoks) + [st["bank_free"][bank], st["cast"], st.get("cast2")], sig=(kc == 7))
        return tok

    for half in range(2):
        tiles = [("kv", tt) for tt in range(32)] + [("own", ot) for ot in range(8)]
        tinfo = {}

        dinfo = {}

        def stage_D(n):
            kind, idx = tiles[n]
            xb = st.setdefault("dtile", 0) % 3
            st["dtile"] += 1
            if kind == "kv":
                ncols = 257
                src = xT_all[:, :, idx * 256:idx * 256 + 257]
            else:
                ncols = 256
                src = xT_own[:, :, idx * 256:(idx + 1) * 256]
            w = list(st["mod_done"][xb] or [])
            if xb == 2:
                w += [st.get("wcast0"), st.get("wcast1")]
            dinfo[n] = (xb, ncols, dma(f"xs{xb}", xs[xb][:, :, 0:ncols], src, waits=w))

        def stage_M(n):
            buf = st["tile"] % 2
            st["tile"] += 1
            xb, ncols, t_x = dinfo[n]
            mt = modulate(buf, xb, ncols, t_x)
            tinfo[n] = (buf, mt)

        def stage_C1(n):
            kind, idx = tiles[n]
            buf, mt = tinfo[n]
            pe_tok = None
            if kind == "kv":
                tt = idx
                for hp in range(2):
                    pe_tok = proj(buf, hp, hp * 128, 256, mt)
                    kk = op("dve", lambda e, hp=hp: e.tensor_copy(
                        out=kT[:, hp, tt * 256:(tt + 1) * 256], in_=psb[hp][:, 0:256]),
                        waits=[pe_tok, st.get("att_pe_done")])
                    st["bank_free"][hp] = kk
                vts = []
                for hp in range(2):
                    pe_tok = proj(buf, 2 + hp, 256 + hp * 128, 257, mt)
                    vk = op("dve", lambda e, hp=hp: e.tensor_copy(
                        out=vtmp[buf][:, hp, 0:257], in_=psb[2 + hp][:, 0:257]),
                        waits=[pe_tok, st["vt_rd"][buf]])
                    st["bank_free"][2 + hp] = vk
                    vts.append(vk)
                st["pe_done"][buf] = pe_tok
                if tt == 31:
                    vts.append(op("dve", lambda e: e.memset(vtmp[buf][:, :, 256:257], 0.0), waits=vts))
                gk = op("pool", lambda e: e.tensor_tensor(
                    out=gTt[buf][:, :, :], in0=vtmp[buf][:, :, 1:257], in1=vtmp[buf][:, :, 0:256],
                    op=ALU.subtract), waits=vts + [st["gt_rd"][buf]])
                st["vt_rd"][buf] = gk
                st["k_done"] = st["bank_free"][1]
                tinfo[n] = (buf, mt, gk)
            else:
                ot = idx
                for hp in range(2):
                    pe_tok = proj(buf, hp, 512 + hp * 128, 256, mt)
                    kk = op("dve", lambda e, hp=hp: e.tensor_copy(
                        out=qT[:, hp, ot * 256:(ot + 1) * 256], in_=psb[hp][:, 0:256]),
                        waits=[pe_tok, st.get("att_pe_done")])
                    st["bank_free"][hp] = kk
                for hp in range(2):
                    pe_tok = proj(buf, 2 + hp, 256 + hp * 128, 256, mt)
                    kk = op("dve", lambda e, hp=hp: e.tensor_copy(
                        out=vown[:, hp, ot * 256:(ot + 1) * 256], in_=psb[2 + hp][:, 0:256]),
                        waits=[pe_tok, st.get("att_ep_done")])
                    st["bank_free"][2 + hp] = kk
                for hp in range(2):
                    pe_tok = proj(buf, 6 + hp, 768 + hp * 128, 256, mt)
                    kk = op("act", lambda e, hp=hp: e.activation(
                        out=szb[:, hp, ot * 256:(ot + 1) * 256], in_=psb[6 + hp][:, 0:256], func=AF.Silu),
                        waits=[pe_tok, st.get("att_ep_done")])
                    st["bank_free"][6 + hp] = kk
                st["pe_done"][buf] = pe_tok
                st["q_done"] = [st["bank_free"][0], st["bank_free"][1], st["bank_free"][2],
                                st["bank_free"][3], st["bank_free"][6], st["bank_free"][7]]

        def stage_C2(n):
            kind, idx = tiles[n]
            if kind != "kv":
                return
            tt = idx
            buf, mt, gk = tinfo[n]
            tb = 4 + buf
            trt = None
            for blk in range(2):
                for hp in range(2):
                    trt = op("pe", lambda e, blk=blk, hp=hp: e.transpose(
                        psbf(tb)[:, (blk * 2 + hp) * 128:(blk * 2 + hp + 1) * 128],
                        gTt[buf][:, hp, blk * 128:(blk + 1) * 128], ident),
                        waits=[gk, st["bank_free"][tb]], sig=(blk == 1 and hp == 1))
            st["gt_rd"][buf] = trt
            gev = op("act", lambda e: e.activation(
                out=g[:, tt * 2:tt * 2 + 2, :], in_=psbf(tb)[:, 0:512].rearrange("p (a b) -> p a b", b=256),
                func=AF.Identity), waits=[trt, st.get("att_pe_done")])
            st["bank_free"][tb] = gev
            st["g_done"] = gev

        NTL = len(tiles)
        if half == 0:
            dinfo.update(pre_d)
        else:
            stage_D(0)
            stage_D(1)
        stage_M(0)
        for n in range(NTL):
            if n + 2 < NTL:
                stage_D(n + 2)
            if n + 1 < NTL:
                stage_M(n + 1)
            stage_C1(n)
            if n >= 1:
                stage_C2(n - 1)
        stage_C2(NTL - 1)

        if STAGE == 2:
            for eng in ("pe", "act", "dve", "pool", "sp"):
                P.wait(eng, list(st["q_done"]) + [st["pe_done"][0], st["pe_done"][1]])
            return
        if half == 0:
            bg.extend(half_weight_tasks(1, cast_waits=[st["pe_done"][0], st["pe_done"][1]]))
        if half == 1:
            ph1_done = [st["pe_done"][0], st["pe_done"][1], st["gt_rd"][0], st["gt_rd"][1], st["vt_rd"][0],
                        st["vt_rd"][1], st["wcast0"], st["wcast1"], st["g_done"]] + \
                list(st["mod_done"][0] or []) + list(st["mod_done"][1] or []) + \
                list(st["mod_done"][2] or []) + list(st["q_done"])
            t_g1 = dma("p3_g1", G1, T["gate_scratch"][0:1, :].partition_broadcast(128), waits=[t_gsc])
            k_g1 = op("pool", lambda e: e.tensor_scalar(out=G1, in0=G1, scalar1=1.0, scalar2=None, op0=ALU.add),
                      waits=[t_g1])
            p3w = {"tw": {}, "wtok": {}}

            def p3_issue(k):
                sb = k % 2
                if k < 8:
                    p3w["tw"][k] = dma(f"w3st{sb}", wst3[sb], T["w_in"][k * 128:(k + 1) * 128, 0:1536],
                                       waits=ph1_done + [p3w["wtok"].get(sb)])
                else:
                    p3w["tw"][k] = dma(f"w3st{sb}", wst3[sb][:, 0:1024],
                                       T["w_out"][(k - 8) * 128:(k - 7) * 128, :], waits=[p3w["wtok"].get(sb)])

            def p3_task(k):
                sb = k % 2
                if k < 8:
                    tok = op("act", lambda e: e.activation(out=wu[:, k, :], in_=wst3[sb], func=AF.Identity),
                             waits=[p3w["tw"][k]] + ph1_done)
                    st["cu"] = tok
                else:
                    tok = op("act", lambda e: e.activation(out=wo[:, k - 8, :], in_=wst3[sb][:, 0:1024],
                                                           func=AF.Identity), waits=[p3w["tw"][k]])
                    st["co_raw"] = tok
                p3w["wtok"][sb] = tok
                if k + 2 < 16:
                    p3_issue(k + 2)

            p3_issue(0)
            p3_issue(1)
            bg.extend([lambda k=k: p3_task(k) for k in range(16)])
        pus = [(m, hp, k) for m in range(16) for hp in range(2) for k in range(m, 16)]
        NPU = len(pus)
        ready = list(st["q_done"]) + [st["g_done"], st["k_done"], st["pe_done"][0], st["pe_done"][1],
                                      st["bank_free"][4], st["bank_free"][5]]
        tk = {}

        def qk(i):
            m, hp, k = pus[i]
            par = i % 2
            for hl in range(2):
                tk[("qk", i, hl)] = op("pe", lambda e, hl=hl: e.matmul(
                    psb[par * 2 + hl][:, 0:512],
                    lhsT=qT[hl * 64:(hl + 1) * 64, hp, m * 128:(m + 1) * 128],
                    rhs=kT[hl * 64:(hl + 1) * 64, hp, k * 512:(k + 1) * 512],
                    start=True, stop=True), waits=ready + [tk.get(("sig", i - 2, hl))])

        def sig(i):
            par = i % 2
            for hl in range(2):
                tk[("sig", i, hl)] = op("act", lambda e, hl=hl: e.activation(
                    out=pbuf[par][hl], in_=psb[par * 2 + hl][:, 0:512], func=AF.Sigmoid, scale=-0.125),
                    waits=[tk[("qk", i, hl)], tk.get(("scan", i - 2, hl))])

        def scan(i):
            m, hp, k = pus[i]
            par = i % 2
            for hl in range(2):
                first = (k == m)
                w = [tk[("sig", i, hl)], tk.get(("tr", i - 3, hl)), tk.get(("scan", i - 2, hl))]
                if not first:
                    w.append(tk[("scan", i - 1, hl)])
                    init = Ebuf[(i - 1) % 3][hl][:, 511:512]
                    d1 = zeros
                else:
                    init = 0.0
                    d1 = dmat
                tk[("scan", i, hl)] = op("dve", lambda e, hl=hl, init=init, d1=d1: e.tensor_tensor_scan(
                    out=Ebuf[i % 3][hl], data0=pbuf[par][hl], data1=d1, initial=init,
                    op0=ALU.mult, op1=ALU.add), waits=w)

        def tr(i):
            par = i % 2
            for hl in range(2):
                t = None
                for blk in range(4):
                    t = op("pe", lambda e, hl=hl, blk=blk: e.transpose(
                        psbf(4 + par)[:, hl * 512 + blk * 128:hl * 512 + (blk + 1) * 128],
                        Ebuf[i % 3][hl][:, blk * 128:(blk + 1) * 128], ident),
                        waits=[tk[("scan", i, hl)], tk.get(("ev", i - 2, 0)), tk.get(("ev", i - 2, 1))],
                        sig=(blk == 3))
                tk[("tr", i, hl)] = t

        def ev(i):
            par = i % 2
            for hl in range(2):
                tk[("ev", i, hl)] = op("act", lambda e, hl=hl: e.activation(
                    out=ETbuf[par][hl], in_=psbf(4 + par)[:, hl * 512:(hl + 1) * 512], func=AF.Identity),
                    waits=[tk[("tr", i, 0)], tk[("tr", i, 1)], tk.get(("wv", i - 2))])

        def wv(i):
            m, hp, k = pus[i]
            par = i % 2
            chain = m * 2 + hp
            ob = 6 + chain % 2
            while pend and pend[0][1] <= chain - 2:
                epilogue(pend.pop(0))
            t = None
            for blk in range(4):
                for hl in range(2):
                    t = op("pe", lambda e, hl=hl, blk=blk: e.matmul(
                        psb[ob][hl * 64:(hl + 1) * 64, 0:128],
                        lhsT=g[:, k * 4 + blk, hp * 128 + hl * 64:hp * 128 + (hl + 1) * 64],
                        rhs=ETbuf[par][hl][:, blk * 128:(blk + 1) * 128],
                        start=(k == m and blk == 0), stop=(k == 15 and blk == 3)),
                        waits=[tk[("ev", i, hl)], tk.get(("ep", chain - 2))], sig=(blk == 3 and hl == 1))
            tk[("wv", i)] = t
            if k == 15:
                pend.append((i, chain, m, hp, ob, t))

        pend = []

        def epilogue(ent):
            i, chain, m, hp, ob, t = ent
            e1 = op("dve", lambda e: e.tensor_tensor(
                out=eptmp[chain % 2], in0=psb[ob][:, 0:128], in1=vown[:, hp, m * 128:(m + 1) * 128],
                op=ALU.add), waits=[t, tk.get(("ep2", chain - 2))])
            tk[("ep", chain)] = e1
            tk[("ep2", chain)] = op("pool", lambda e: e.tensor_tensor(
                out=obT[:, half * 2 + hp, m * 128:(m + 1) * 128], in0=eptmp[chain % 2],
                in1=szb[:, hp, m * 128:(m + 1) * 128], op=ALU.mult), waits=[e1])
            st["att_ep_done"] = tk[("ep2", chain)]

        def flush_ep(upto):
            while pend and pend[0][0] <= upto:
                epilogue(pend.pop(0))

        qk(0)
        if NPU > 1:
            qk(1)
        sig(0)
        for i in range(NPU):
            if i + 1 < NPU:
                sig(i + 1)
            scan(i)
            flush_ep(i - 3)
            tr(i)
            if i + 2 < NPU:
                qk(i + 2)
            ev(i)
            if bg and i % 12 == 6:
                bg.pop(0)()
            if i >= 1:
                wv(i - 1)
        wv(NPU - 1)
        flush_ep(NPU)
        while bg:
            bg.pop(0)()
        st["att_pe_done"] = tk[("wv", NPU - 1)]
        st["pe_all"] = tk[("wv", NPU - 1)]
        for b in range(8):
            st["bank_free"][b] = st["att_ep_done"] if b >= 6 else tk[("ev", NPU - 1, 1)]

        if STAGE == 3:
            for eng in ("pe", "act", "dve", "pool", "sp"):
                P.wait(eng, [st["att_ep_done"], st["att_pe_done"], tk[("ev", NPU - 1, 1)]])
            return
    fin = [st["att_ep_done"], st["att_pe_done"], tk[("ev", NPU - 1, 1)], st["wcast0"], st["wcast1"],
           st["cu"], st["co_raw"]]
    for eng in ("pe", "act", "dve", "pool", "sp"):
        P.wait(eng, fin)

    A.reset(p3_head_end)
    lng = A.alloc([1024], F32)
    lnb = A.alloc([1024], F32)
    lnvg = A.alloc([512], F32)
    lnvb = A.alloc([512], F32)
    bsT = A.alloc([4, 128], F32)
    wsT = A.alloc([8, 128], BF16)
    ws32 = A.alloc([8, 128], F32)
    sel = A.alloc([4, 128], F32, parts=8)
    bs8 = A.alloc([128], F32, parts=8)
    xs3_1 = A.alloc([8, 512], F32)
    xs3 = [xs3_1, xs3_1]
    hT3 = [A.alloc([8, 512], BF16), A.alloc([8, 512], BF16)]
    uT = [A.alloc([4, 512], F32), A.alloc([4, 512], F32)]
    zaT_1 = A.alloc([4, 512], F32)
    zaT = [zaT_1, zaT_1]
    gv = [A.alloc([512], F32) for _ in range(3)]
    vnrm = A.alloc([512], F32)
    vn = [A.alloc([512], BF16) for _ in range(3)]
    mx = A.alloc([4, 128], F32)
    oaT = [A.alloc([4, 128], BF16), A.alloc([4, 128], BF16)]
    xtok = [A.alloc([1024], F32), A.alloc([1024], F32)]
    rr = A.alloc([1024], F32)
    yo = [A.alloc([1024], F32), A.alloc([1024], F32)]
    stats = A.alloc([32], F32)

    pre_tl0 = dma("xs3", xs3[0], xT_own[:, :, 0:512])
    cu = st["cu"]
    co = None
    for kc in range(8):
        co = op("dve", lambda e, kc=kc: e.tensor_tensor(out=wo[:, kc, :], in0=wo[:, kc, :], in1=G1, op=ALU.mult),
                waits=[st["co_raw"], k_g1])
    t_lng = dma("p3_lng", lng, T["ln_g"][0:1, :].partition_broadcast(128))
    t_lnb = dma("p3_lnb", lnb, T["ln_b"][0:1, :].partition_broadcast(128))
    t_lvg = dma("p3_lvg", lnvg, T["ln_v_g"][0:1, :].partition_broadcast(128))
    t_lvb = dma("p3_lvb", lnvb, T["ln_v_b"][0:1, :].partition_broadcast(128))
    t_ws = dma("p3_ws", ws32, T["wsT"].rearrange("h j i -> j h i"))
    t_sel = dma("p3_sel", sel, T["sel"])
    t_bs = dma("p3_bs", bs8, T["bs_rev"][:, :])
    k_ws = op("dve", lambda e: e.tensor_copy(out=wsT, in_=ws32), waits=[t_ws])
    k_ws = op("dve", lambda e: e.memset(wsT[0:64, :, 64:128], 0.0), waits=[k_ws])
    tb_ = None
    for hp in range(4):
        tb_ = op("pe", lambda e, hp=hp: e.matmul(
            psb[0][:, hp * 128:(hp + 1) * 128], lhsT=sel[:, hp, :], rhs=bs8[:, :], start=True, stop=True),
            waits=[t_sel, t_bs], sig=(hp == 3))
    k_bs = op("dve", lambda e: e.tensor_copy(out=bsT, in_=psb[0][:, 0:512].rearrange("p (a b) -> p a b", b=128)),
              waits=[tb_])
    s3 = {"hT_rd": [None, None], "xs_rd": [None, None], "ps_free": [k_bs] + [None] * 7,
          "uT_rd": [None, None], "gv_rd": [None, None, None], "vn_rd": [None, None, None], "oa_rd": [None, None],
          "x_rd": [None, None], "yo_rd": [None, None], "mx_rd": None, "rr_rd": None, "vnrm_rd": None,
          "stA_rd": None, "stC_rd": None}
    out_toks = []
    tile_mt = {}
    tile_uz = {}
    blk_oa = {}
    blk_vn = {}
    sa1 = {}
    scs = {}

    tl_dma = {}

    def TLd(Tt):
        tb = Tt % 2
        if Tt == 0:
            tl_dma[0] = pre_tl0
            return
        tl_dma[Tt] = dma("xs3", xs3[tb], xT_own[:, :, Tt * 512:(Tt + 1) * 512], waits=s3["xs_rd"][0] or [])

    def TL(Tt):
        tb = Tt % 2
        if Tt not in tl_dma:
            TLd(Tt)
        t_x = tl_dma[Tt]
        mtoks = []
        for c in range(8):
            w = [t_x, s3["hT_rd"][tb]]
            mtoks.append(op("act", lambda e, c=c: e.activation(
                out=hT3[tb][:, c, :], in_=xs3[tb][:, c, :], func=AF.Identity,
                scale=sc1[:, c:c + 1], bias=sh[:, c:c + 1]), waits=w))
        s3["xs_rd"][0] = [mtoks[4], mtoks[7]]
        tile_mt[Tt] = [mtoks[4], mtoks[7]]

    tu_toks = {}

    def TUp(Tt, part):
        tb = Tt % 2
        mt = tile_mt[Tt]
        c0, dst, fn = ((0, uT[tb], AF.Gelu_apprx_tanh), (1024, zaT[tb], AF.Silu))[part // 2]
        for fc in ((part % 2) * 2, (part % 2) * 2 + 1):
            bank = fc % 2
            t = None
            for kc in range(8):
                t = op("pe", lambda e, kc=kc, fc=fc, c0=c0: e.matmul(
                    psb[bank][:, 0:512], lhsT=wu[:, kc, c0 + fc * 128:c0 + (fc + 1) * 128],
                    rhs=hT3[tb][:, kc, :], start=(kc == 0), stop=(kc == 7)),
                    waits=mt + [cu, s3["ps_free"][bank]], sig=(kc == 7))
            k = op("act", lambda e, fc=fc, dst=dst, fn=fn: e.activation(
                out=dst[:, fc, :], in_=psb[bank][:, 0:512], func=fn),
                waits=[t, s3["uT_rd"][tb], s3.get("za_rd")])
            s3["ps_free"][bank] = k
            tu_toks.setdefault(Tt, {})[(part // 2, fc)] = k
            if part >= 2:
                tile_uz[Tt] = op("pool", lambda e, fc=fc: e.tensor_tensor(
                    out=uT[tb][:, fc, :], in0=uT[tb][:, fc, :], in1=zaT[tb][:, fc, :], op=ALU.mult),
                    waits=[k, tu_toks[Tt][(0, fc)]])
                s3["za_rd"] = tile_uz[Tt]

    def TU(Tt):
        for part in range(4):
            TUp(Tt, part)

    def SA1a(B):
        Tt, blk = B // 4, B % 4
        tb = Tt % 2
        pb = B % 3
        mt = tile_mt[Tt]
        t = None
        for kc in range(8):
            t = op("pe", lambda e, kc=kc: e.matmul(
                psb[2][:, 0:512], lhsT=hT3[tb][:, kc, blk * 128:(blk + 1) * 128], rhs=wu[:, kc, 512:1024],
                start=(kc == 0), stop=(kc == 7)), waits=mt + [cu, s3["ps_free"][2]], sig=(kc == 7))
        if blk == 3:
            s3["hT_rd"][tb] = t
        k_gv = op("act", lambda e: e.activation(out=gv[pb], in_=psb[2][:, 0:512], func=AF.Gelu_apprx_tanh),
                  waits=[t, s3["gv_rd"][pb]])
        s3["ps_free"][2] = k_gv
        k_st = op("dve", lambda e: e.bn_stats(out=stats[:, 0:6], in_=gv[pb]), waits=[k_gv, s3["stA_rd"]])
        k_ag = op("dve", lambda e: e.bn_aggr(out=stats[:, 6:8], in_=stats[:, 0:6]), waits=[k_st])
        k_e = op("pool", lambda e: e.tensor_scalar(out=stats[:, 8:9], in0=stats[:, 7:8], scalar1=LN_EPS,
                                                   scalar2=None, op0=ALU.add), waits=[k_ag])
        k_sd = op("pool", lambda e: e.tensor_tensor(out=stats[:, 9:10], in0=stats[:, 8:9], in1=mhalf[:, 0:1],
                                                    op=ALU.pow), waits=[k_e])
        sa1[B] = k_sd

    def SA1b(B):
        pb = B % 3
        k_sd = sa1[B]
        k_rs = k_sd
        k_n = op("dve", lambda e: e.scalar_tensor_tensor(
            out=vnrm, in0=gv[pb], scalar=stats[:, 6:7], in1=lnvg, op0=ALU.subtract, op1=ALU.mult),
            waits=[k_rs, t_lvg, s3["vnrm_rd"]])
        s3["gv_rd"][pb] = k_n
        k_vn = op("dve", lambda e: e.scalar_tensor_tensor(
            out=vn[pb], in0=vnrm, scalar=stats[:, 9:10], in1=lnvb, op0=ALU.mult, op1=ALU.add),
            waits=[k_n, t_lvb, s3["vn_rd"][pb]])
        s3["vnrm_rd"] = k_vn
        s3["stA_rd"] = k_vn
        blk_vn[B] = k_vn

    def SA2(B):
        Tt, blk = B // 4, B % 4
        tb = Tt % 2
        pb = B % 2
        p3 = B % 3
        k_vn = blk_vn[B]
        t = None
        for hp in range(4):
            for hl in range(2):
                t = op("pe", lambda e, hp=hp, hl=hl: e.matmul(
                    psb[3][hl * 64:(hl + 1) * 64, hp * 128:(hp + 1) * 128],
                    lhsT=vn[p3][:, (2 * hp + hl) * 64:(2 * hp + hl + 1) * 64], rhs=wsT[:, 2 * hp + hl, :],
                    start=True, stop=True), waits=[k_vn, k_ws, s3["ps_free"][3]],
                    sig=(hp == 3 and hl == 1))
        s3["vn_rd"][p3] = t
        k_mx = op("dve", lambda e: e.tensor_tensor(
            out=mx[:, :, :], in0=psb[3][:, 0:512].rearrange("p (a b) -> p a b", b=128), in1=bsT[:, :, :],
            op=ALU.add), waits=[t, k_bs, s3["mx_rd"]])
        s3["ps_free"][3] = k_mx
        k_oa = op("dve", lambda e: e.tensor_tensor(
            out=oaT[pb][:, :, :], in0=mx[:, :, :], in1=uT[tb][:, :, blk * 128:(blk + 1) * 128], op=ALU.mult),
            waits=[k_mx, tile_uz[Tt], s3["oa_rd"][pb]])
        s3["mx_rd"] = k_oa
        if blk == 3:
            s3["uT_rd"][tb] = k_oa
        blk_oa[B] = k_oa

    def SCa(B):
        pb = B % 2
        k_oa = blk_oa[B]
        ty = [None, None]
        for nh in range(2):
            bank = 4 + nh
            t = None
            for fc in range(8):
                lhs = oaT[pb][:, fc, :] if fc < 4 else obT[:, fc - 4, B * 128:(B + 1) * 128]
                t = op("pe", lambda e, fc=fc, nh=nh, lhs=lhs: e.matmul(
                    psb[bank][:, 0:512], lhsT=lhs, rhs=wo[:, fc, nh * 512:(nh + 1) * 512],
                    start=(fc == 0), stop=(fc == 7)), waits=[k_oa, co, s3["ps_free"][bank]], sig=(fc == 7))
            ty[nh] = t
        s3["oa_rd"][pb] = ty[1]
        t_xt = dma(f"xtok{pb}", xtok[pb], T["x_own"][B * 128:(B + 1) * 128, :], waits=[s3["x_rd"][pb]])
        k_r = None
        k_s = None
        for nh in range(2):
            k_r = op("dve", lambda e, nh=nh: e.scalar_tensor_tensor(
                out=rr[:, nh * 512:(nh + 1) * 512], in0=xtok[pb][:, nh * 512:(nh + 1) * 512], scalar=ALPHA,
                in1=psb[4 + nh][:, 0:512], op0=ALU.mult, op1=ALU.add), waits=[ty[nh], t_xt, s3["rr_rd"]])
            s3["ps_free"][4 + nh] = k_r
            k_s = op("dve", lambda e, nh=nh: e.bn_stats(out=stats[:, 12 + nh * 6:18 + nh * 6],
                                                        in_=rr[:, nh * 512:(nh + 1) * 512]),
                     waits=[k_r, s3["stC_rd"]])
        s3["x_rd"][pb] = k_r
        k_ag = op("dve", lambda e: e.bn_aggr(out=stats[:, 24:26], in_=stats[:, 12:24]), waits=[k_s])
        k_e = op("pool", lambda e: e.tensor_scalar(out=stats[:, 26:27], in0=stats[:, 25:26], scalar1=LN_EPS,
                                                   scalar2=None, op0=ALU.add), waits=[k_ag])
        k_sd = op("pool", lambda e: e.tensor_tensor(out=stats[:, 27:28], in0=stats[:, 26:27], in1=mhalf[:, 0:1],
                                                    op=ALU.pow), waits=[k_e])
        scs[B] = k_sd

    def SCb(B):
        pb = B % 2
        k_sd = scs[B]
        k_y = op("dve", lambda e: e.scalar_tensor_tensor(
            out=yo[pb], in0=rr, scalar=stats[:, 24:25], in1=lng, op0=ALU.subtract, op1=ALU.mult),
            waits=[k_sd, t_lng, s3["yo_rd"][pb]])
        s3["rr_rd"] = k_y
        s3["stC_rd"] = k_y
        k_y3 = op("dve", lambda e: e.scalar_tensor_tensor(
            out=yo[pb], in0=yo[pb], scalar=stats[:, 27:28], in1=lnb, op0=ALU.mult, op1=ALU.add),
            waits=[k_y, t_lnb])
        s3["stC_rd"] = k_y3
        t_o = dma(f"yout{pb}", T["out_own"][B * 128:(B + 1) * 128, :], yo[pb], waits=[k_y3])
        s3["yo_rd"][pb] = t_o
        out_toks.append(t_o)

    TL(0)
    TL(1)
    TU(0)
    SA1a(0)
    SA1b(0)
    SA1a(1)
    SA1b(1)
    for B in range(16):
        Tt, blk = B // 4, B % 4
        if blk == 0 and Tt + 2 < 4:
            TLd(Tt + 2)
        if Tt + 1 < 4:
            TUp(Tt + 1, blk)
        if B + 2 < 16:
            SA1a(B + 2)
        SA2(B)
        if B >= 1:
            SCa(B - 1)
        if B + 2 < 16:
            SA1b(B + 2)
        if blk == 2 and Tt + 2 < 4:
            TL(Tt + 2)
        if B >= 1:
            SCb(B - 1)
    SCa(15)
    SCb(15)
    P.wait("sp", out_toks[-2:])
    P.wait("act", out_toks[-2:])


_IN_SPECS = [
    ("xT_all", [DM, S + 1]), ("xT_own", [DM, NOWN]), ("x_own", [NOWN, DM]),
    ("w_in", [DM, 3584]), ("w_out", [DM, DM]), ("w_ada", [DM, 3072]), ("b_ada", [1, 3072]),
    ("cT", [128, 8]), ("ln_v_g", [1, 512]), ("ln_v_b", [1, 512]), ("ln_g", [1, DM]), ("ln_b", [1, DM]),
    ("wsT", [8, 128, 128]), ("bs_rev", [8, 128]), ("sel", [8, 4, 128]), ("dmat", [128, 512]),
    ("ident", [128, 128]),
]


def build_nc():
    nc = bass.Bass("TRN2", target_bir_lowering=False)
    T = {}
    for name, shape in _IN_SPECS:
        T[name] = nc.dram_tensor(name, shape, F32, kind="ExternalInput").ap()
    T["out_own"] = nc.dram_tensor("out_own", [NOWN, DM], F32, kind="ExternalOutput").ap()
    T["gate_scratch"] = nc.dram_tensor("gate_scratch", [1, DM], F32, kind="Internal").ap()
    with contextlib.ExitStack() as es:
        arena = es.enter_context(nc.sbuf_tensor("arena", [128, ARENA_WORDS], F32))
        psb = [es.enter_context(nc.psum_tensor(f"psb{i}", [128, 512], F32)) for i in range(8)]
        sems = {k: es.enter_context(nc.semaphore(f"s_{k}")) for k in ("pe", "act", "dve", "pool")}
        dsems = [es.enter_context(nc.semaphore(f"d_{i}")) for i in range(N_DSEM)]
        block = es.enter_context(nc.Block())
        arena_ap = arena[:, :]
        psb_ap = [p[:, :] for p in psb]

        @block.tensor
        def _(e):
            generate(Prog("pe", e, sems, dsems), nc, T, arena_ap, psb_ap)

        @block.scalar
        def _(e):
            generate(Prog("act", e, sems, dsems), nc, T, arena_ap, psb_ap)

        @block.vector
        def _(e):
            generate(Prog("dve", e, sems, dsems), nc, T, arena_ap, psb_ap)

        @block.gpsimd
        def _(e):
            generate(Prog("pool", e, sems, dsems), nc, T, arena_ap, psb_ap)

        @block.sync
        def _(e):
            generate(Prog("sp", e, sems, dsems), nc, T, arena_ap, psb_ap)
    return nc


def _own_idx(j):
    return (np.arange(16)[:, None] * 512 + 128 * j + np.arange(128)[None, :]).reshape(-1)


def kernel(x, c, w_ada, b_ada, w_in, ln_v_g, ln_v_b, w_spatial, b_spatial, w_out, ln_g, ln_b):
    f = lambda a: np.ascontiguousarray(np.asarray(a, dtype=np.float32))
    x = f(x); c = f(c)
    wsT = f(np.transpose(f(w_spatial)[0][:, ::-1, ::-1], (0, 2, 1)))
    bs_rev = f(f(b_spatial)[0][:, ::-1])
    sel = np.zeros((8, 4, 128), np.float32)
    for h in range(8):
        sel[h, h // 2, (h % 2) * 64:(h % 2) * 64 + 64] = 1.0
    ident = np.eye(128, dtype=np.float32)
    shared = {
        "w_in": f(w_in)[0], "w_out": f(w_out)[0], "w_ada": f(w_ada)[0], "b_ada": f(b_ada)[0][None, :],
        "ln_v_g": f(ln_v_g)[0][None, :], "ln_v_b": f(ln_v_b)[0][None, :],
        "ln_g": f(ln_g)[0][None, :], "ln_b": f(ln_b)[0][None, :],
        "wsT": wsT, "bs_rev": bs_rev, "sel": sel, "ident": ident,
    }
    in_maps = []
    for core in range(8):
        b, j = core // 4, core % 4
        xr = x[b, ::-1, :]
        xT_all = np.zeros((DM, S + 1), np.float32)
        xT_all[:, :S] = xr.T
        idx = _own_idx(j)
        x_own = f(xr[idx])
        dmat = np.zeros((128, 512), np.float32)
        dmat[np.arange(128), 128 * j + np.arange(128)] = 1.0
        m = dict(shared)
        m.update({"xT_all": xT_all, "xT_own": f(x_own.T), "x_own": x_own,
                  "cT": f(c[b].reshape(8, 128).T), "dmat": dmat})
        in_maps.append(m)
    nc = build_nc()
    res = run_bass_kernel_spmd(nc, in_maps, core_ids=list(range(8)))
    out = np.zeros((2, S, DM), np.float32)
    for core in range(8):
        b, j = core // 4, core % 4
        o_rev = res.results[core]["out_own"]
        out[b, S - 1 - _own_idx(j), :] = o_rev
    return out
```

```python
import contextlib
import numpy as np
import concourse.bass as bass
import concourse.mybir as mybir
from concourse.bass_utils import run_bass_kernel_spmd

F32 = mybir.dt.float32
BF16 = mybir.dt.bfloat16
ALU = mybir.AluOpType
AF = mybir.ActivationFunctionType

S = 8192
DM = 1024
NOWN = 2048
ALPHA = 2.0 ** 0.25
LN_EPS = 1e-5
ARENA_WORDS = 51200
N_DSEM = 40
STAGE = 99


class Prog:
    def __init__(self, cur, eobj, sems, dsems):
        self.cur = cur
        self.e = eobj
        self.sems = sems
        self.dsems = dsems
        self.cnt = {k: 0 for k in sems}
        self.waited = {}
        self.dmap = {}
        self.dcnt = {}

    def _sem(self, key):
        if key in self.sems:
            return self.sems[key]
        return self.dsems[self.dmap[key]]

    def wait(self, eng, toks):
        for t in toks:
            if t is None:
                continue
            key, val = t
            if val > self.waited.get((eng, key), 0):
                self.waited[(eng, key)] = val
                if eng == self.cur:
                    self.e.wait_ge(self._sem(key), val)

    def op(self, eng, fn, waits=(), sig=True):
        self.wait(eng, waits)
        tok = None
        if sig:
            self.cnt[eng] += 1
            tok = (eng, self.cnt[eng])
        if eng == self.cur:
            ins = fn(self.e)
            if sig:
                ins.then_inc(self.sems[eng], 1)
        return tok

    def dma(self, slot, out, in_, waits=(), q="sp"):
        self.wait(q, waits)
        if slot not in self.dmap:
            self.dmap[slot] = len(self.dmap)
            self.dcnt[slot] = 0
            assert len(self.dmap) <= len(self.dsems), "out of dma semaphores"
        self.dcnt[slot] += 16
        if q == self.cur:
            self.e.dma_start(out=out, in_=in_).then_inc(self.dsems[self.dmap[slot]], 16)
        return (slot, self.dcnt[slot])


class Arena:
    def __init__(self, ap):
        self.ap = ap
        self.off = 0
        self.hi = 0

    def mark(self):
        return self.off

    def reset(self, off):
        self.off = off

    def alloc(self, shape_free, dtype, parts=128):
        esz = 2 if dtype == BF16 else 4
        n = int(np.prod(shape_free)) * esz
        n = (n + 31) // 32 * 32
        off = self.off
        self.off += n
        self.hi = max(self.hi, self.off)
        assert self.off <= ARENA_WORDS * 4, f"arena overflow {self.off}"
        a = self.ap[0:parts, off // 4:(off + n) // 4]
        if dtype == BF16:
            a = a.bitcast(BF16)
        tot = int(np.prod(shape_free))
        a = a[:, 0:tot]
        if len(shape_free) == 2:
            a = a.rearrange("p (a b) -> p a b", b=shape_free[1])
        elif len(shape_free) == 3:
            a = a.rearrange("p (a b c) -> p a b c", b=shape_free[1], c=shape_free[2])
        return a


def generate(P, nc, T, arena_ap, psb):
    A = Arena(arena_ap)
    op = P.op
    dma = P.dma

    def psbf(b):
        return psb[b].bitcast(BF16)

    ident = A.alloc([128], BF16)
    zeros = A.alloc([512], F32)
    dmat = A.alloc([512], F32)
    sc1 = A.alloc([8], F32)
    sh = A.alloc([8], F32)
    epst = A.alloc([8], F32)
    ones_r = A.alloc([128], F32)
    mhalf = A.alloc([8], F32)
    obT = A.alloc([4, NOWN], BF16)
    G1 = A.alloc([1024], F32)
    base_mark = A.mark()
    wu = A.alloc([8, 1536], BF16)
    wo = A.alloc([8, 1024], BF16)
    wst3 = [A.alloc([1536], F32), A.alloc([1536], F32)]
    p3_head_end = A.mark()
    A.reset(base_mark)

    A.reset(base_mark)
    wh = A.alloc([8, 1024], BF16)
    wsx = A.alloc([2112], F32)
    wstage = [wsx[:, 0:1024], wsx[:, 1024:2048]]
    xs = [A.alloc([8, 258], F32), A.alloc([8, 258], F32),
          wsx[:, 0:2064].rearrange("p (a b) -> p a b", b=258)]
    hTt = [A.alloc([8, 258], BF16), A.alloc([8, 258], BF16)]
    vtmp = [A.alloc([2, 258], F32), A.alloc([2, 258], F32)]
    gTt = [A.alloc([2, 256], BF16), A.alloc([2, 256], BF16)]
    assert A.mark() >= p3_head_end, (A.mark(), p3_head_end)
    setup_mark = A.mark()
    kT = A.alloc([2, S], BF16)
    g = A.alloc([64, 256], BF16)
    qT = A.alloc([2, NOWN], BF16)
    vown = A.alloc([2, NOWN], F32)
    szb = A.alloc([2, NOWN], BF16)
    pbuf = [[A.alloc([512], F32) for _ in range(2)] for _ in range(2)]
    Ebuf = [[A.alloc([512], BF16) for _ in range(2)] for _ in range(3)]
    ETbuf = [[A.alloc([512], BF16) for _ in range(2)] for _ in range(2)]
    eptmp = [A.alloc([128], F32), A.alloc([128], F32)]

    xT_all = T["xT_all"].rearrange("(c p) t -> p c t", p=128)
    xT_own = T["xT_own"].rearrange("(c p) t -> p c t", p=128)

    st = {"mod_done": [None, None, None], "pe_done": [None, None], "gt_rd": [None, None],
          "vt_rd": [None, None], "bank_free": [None] * 8, "cast": None, "tile": 0}

    bg = []

    def half_weight_tasks(half, cast_waits=()):
        tws = {}

        def issue(kc):
            sb = kc % 2
            tw = None
            for gi, c0 in enumerate((2048, 2560, 1536, 3072)):
                tw = dma(f"wst{sb}", wstage[sb][:, gi * 256:(gi + 1) * 256],
                         T["w_in"][kc * 128:(kc + 1) * 128, c0 + half * 256:c0 + half * 256 + 256],
                         waits=[st.get(f"wcast{sb}")] + list(st["mod_done"][2] or []))
            tws[kc] = tw

        def task(kc):
            sb = kc % 2
            st[f"wcast{sb}"] = op("act", lambda e: e.activation(
                out=wh[:, kc, :], in_=wstage[sb], func=AF.Identity), waits=[tws[kc]] + list(cast_waits))
            if kc + 2 < 8:
                issue(kc + 2)
            if kc == 7:
                st["cast"] = st["wcast1"]
                st["cast2"] = st["wcast0"]

        issue(0)
        issue(1)
        return [lambda kc=kc: task(kc) for kc in range(8)]

    frame_end = A.mark()
    A.reset(setup_mark)
    ident32 = A.alloc([128], F32)
    cT = A.alloc([8], F32)
    bada = A.alloc([3072], F32, parts=1)
    modrow = A.alloc([3072], F32, parts=1)
    NWB = 6
    wst = [A.alloc([3072], F32) for _ in range(NWB)]

    t_id = dma("c_ident", ident32, T["ident"][:, :])
    t_dm = dma("c_dmat", dmat, T["dmat"][:, :])
    t_c = dma("c_cT", cT, T["cT"][:, :])
    t_ba = dma("c_bada", bada, T["b_ada"][0:1, :])
    k_z = op("dve", lambda e: e.memset(zeros, 0.0))
    k_one = op("dve", lambda e: e.memset(ones_r, 1.0))
    k_eps = op("dve", lambda e: e.memset(epst, LN_EPS))
    k_mh = op("dve", lambda e: e.memset(mhalf, -0.5))
    k_ident = op("dve", lambda e: e.tensor_copy(out=ident, in_=ident32), waits=[t_id])

    pe_last = [None] * NWB
    mm_tok = None
    for kc in range(8):
        wb = kc % NWB
        t_w = dma(f"wada{wb}", wst[wb], T["w_ada"][kc * 128:(kc + 1) * 128, :], waits=[pe_last[wb]])
        for ct in range(6):
            mm_tok = op("pe", lambda e, kc=kc, ct=ct, wb=wb: e.matmul(
                psb[ct][0:1, 0:512], lhsT=cT[:, kc:kc + 1], rhs=wst[wb][:, ct * 512:(ct + 1) * 512],
                start=(kc == 0), stop=(kc == 7)), waits=[t_w, t_c], sig=(ct == 5))
        pe_last[wb] = mm_tok
    pre_d = {}
    for n in range(2):
        pre_d[n] = (n, 257, dma(f"xs{n}", xs[n][:, :, 0:257], xT_all[:, :, n * 256:n * 256 + 257]))
    st["dtile"] = 2
    for _t in half_weight_tasks(0):
        _t()
    k_mod = None
    for ct in range(6):
        k_mod = op("dve", lambda e, ct=ct: e.tensor_tensor(
            out=modrow[:, ct * 512:(ct + 1) * 512], in0=psb[ct][0:1, 0:512],
            in1=bada[:, ct * 512:(ct + 1) * 512], op=ALU.add), waits=[mm_tok, t_ba])
    t_gsc = dma("gate_sc", T["gate_scratch"][0:1, :], modrow[:, 2048:3072], waits=[k_mod])
    tp = None
    for i in range(16):
        tp = op("pe", lambda e, i=i: e.matmul(
            psb[6][:, i:i + 1], lhsT=modrow[:, i * 128:(i + 1) * 128], rhs=ones_r[0:1, 0:1],
            start=True, stop=True), waits=[k_mod, k_one], sig=(i == 15))
    k_sh = op("dve", lambda e: e.tensor_copy(out=sh, in_=psb[6][:, 0:8]), waits=[tp])
    k_sc = op("dve", lambda e: e.tensor_scalar(out=sc1, in0=psb[6][:, 8:16], scalar1=1.0, scalar2=None,
                                               op0=ALU.add), waits=[tp])
    setup_done = [k_sc, k_sh, k_ident, k_z, k_eps, k_mh, t_dm, t_gsc, tp]
    assert A.mark() <= frame_end
    for eng in ("pe", "act", "dve", "pool"):
        P.wait(eng, setup_done)
    if STAGE == 0:
        return

    def modulate(buf, xb, ncols, t_x):
        toks = []
        for c in range(8):
            w = [t_x, st["pe_done"][buf]]
            if c < 5:
                toks.append(op("act", lambda e, c=c: e.activation(
                    out=hTt[buf][:, c, 0:ncols], in_=xs[xb][:, c, 0:ncols], func=AF.Identity,
                    scale=sc1[:, c:c + 1], bias=sh[:, c:c + 1]), waits=w))
            else:
                toks.append(op("pool", lambda e, c=c: e.tensor_scalar(
                    out=hTt[buf][:, c, 0:ncols], in0=xs[xb][:, c, 0:ncols],
                    scalar1=sc1[:, c:c + 1], scalar2=sh[:, c:c + 1], op0=ALU.mult, op1=ALU.add), waits=w))
        st["mod_done"][xb] = [toks[4], toks[7]]
        return [toks[4], toks[7]]

    def proj(buf, bank, col0, ncols, mtoks):
        tok = None
        for kc in range(8):
            tok = op("pe", lambda e, kc=kc: e.matmul(
                psb[bank][:, 0:ncols], lhsT=wh[:, kc, col0:col0 + 128], rhs=hTt[buf][:, kc, 0:ncols],
                start=(kc == 0), stop=(kc == 7)),
                waits=list(mtoks) + [st["bank_free"][bank], st["cast"], st.get("cast2")], sig=(kc == 7))
        return tok

    for half in range(2):
        tiles = [("kv", tt) for tt in range(32)] + [("own", ot) for ot in range(8)]
        tinfo = {}

        dinfo = {}

        def stage_D(n):
            kind, idx = tiles[n]
            xb = st.setdefault("dtile", 0) % 3
            st["dtile"] += 1
            if kind == "kv":
                ncols = 257
                src = xT_all[:, :, idx * 256:idx * 256 + 257]
            else:
                ncols = 256
                src = xT_own[:, :, idx * 256:(idx + 1) * 256]
            w = list(st["mod_done"][xb] or [])
            if xb == 2:
                w += [st.get("wcast0"), st.get("wcast1")]
            dinfo[n] = (xb, ncols, dma(f"xs{xb}", xs[xb][:, :, 0:ncols], src, waits=w))

        def stage_M(n):
            buf = st["tile"] % 2
            st["tile"] += 1
            xb, ncols, t_x = dinfo[n]
            mt = modulate(buf, xb, ncols, t_x)
            tinfo[n] = (buf, mt)

        def stage_C1(n):
            kind, idx = tiles[n]
            buf, mt = tinfo[n]
            pe_tok = None
            if kind == "kv":
                tt = idx
                for hp in range(2):
                    pe_tok = proj(buf, hp, hp * 128, 256, mt)
                    kk = op("dve", lambda e, hp=hp: e.tensor_copy(
                        out=kT[:, hp, tt * 256:(tt + 1) * 256], in_=psb[hp][:, 0:256]),
                        waits=[pe_tok, st.get("att_pe_done")])
                    st["bank_free"][hp] = kk
                vts = []
                for hp in range(2):
                    pe_tok = proj(buf, 2 + hp, 256 + hp * 128, 257, mt)
                    vk = op("dve", lambda e, hp=hp: e.tensor_copy(
                        out=vtmp[buf][:, hp, 0:257], in_=psb[2 + hp][:, 0:257]),
                        waits=[pe_tok, st["vt_rd"][buf]])
                    st["bank_free"][2 + hp] = vk
                    vts.append(vk)
                st["pe_done"][buf] = pe_tok
                if tt == 31:
                    vts.append(op("dve", lambda e: e.memset(vtmp[buf][:, :, 256:257], 0.0), waits=vts))
                gk = op("pool", lambda e: e.tensor_tensor(
                    out=gTt[buf][:, :, :], in0=vtmp[buf][:, :, 1:257], in1=vtmp[buf][:, :, 0:256],
                    op=ALU.subtract), waits=vts + [st["gt_rd"][buf]])
                st["vt_rd"][buf] = gk
                st["k_done"] = st["bank_free"][1]
                tinfo[n] = (buf, mt, gk)
            else:
                ot = idx
                for hp in range(2):
                    pe_tok = proj(buf, hp, 512 + hp * 128, 256, mt)
                    kk = op("dve", lambda e, hp=hp: e.tensor_copy(
                        out=qT[:, hp, ot * 256:(ot + 1) * 256], in_=psb[hp][:, 0:256]),
                        waits=[pe_tok, st.get("att_pe_done")])
                    st["bank_free"][hp] = kk
                for hp in range(2):
                    pe_tok = proj(buf, 2 + hp, 256 + hp * 128, 256, mt)
                    kk = op("dve", lambda e, hp=hp: e.tensor_copy(
                        out=vown[:, hp, ot * 256:(ot + 1) * 256], in_=psb[2 + hp][:, 0:256]),
                        waits=[pe_tok, st.get("att_ep_done")])
                    st["bank_free"][2 + hp] = kk
                for hp in range(2):
                    pe_tok = proj(buf, 6 + hp, 768 + hp * 128, 256, mt)
                    kk = op("act", lambda e, hp=hp: e.activation(
                        out=szb[:, hp, ot * 256:(ot + 1) * 256], in_=psb[6 + hp][:, 0:256], func=AF.Silu),
                        waits=[pe_tok, st.get("att_ep_done")])
                    st["bank_free"][6 + hp] = kk
                st["pe_done"][buf] = pe_tok
                st["q_done"] = [st["bank_free"][0], st["bank_free"][1], st["bank_free"][2],
                                st["bank_free"][3], st["bank_free"][6], st["bank_free"][7]]

        def stage_C2(n):
            kind, idx = tiles[n]
            if kind != "kv":
                return
            tt = idx
            buf, mt, gk = tinfo[n]
            tb = 4 + buf
            trt = None
            for blk in range(2):
                for hp in range(2):
                    trt = op("pe", lambda e, blk=blk, hp=hp: e.transpose(
                        psbf(tb)[:, (blk * 2 + hp) * 128:(blk * 2 + hp + 1) * 128],
                        gTt[buf][:, hp, blk * 128:(blk + 1) * 128], ident),
                        waits=[gk, st["bank_free"][tb]], sig=(blk == 1 and hp == 1))
            st["gt_rd"][buf] = trt
            gev = op("act", lambda e: e.activation(
                out=g[:, tt * 2:tt * 2 + 2, :], in_=psbf(tb)[:, 0:512].rearrange("p (a b) -> p a b", b=256),
                func=AF.Identity), waits=[trt, st.get("att_pe_done")])
            st["bank_free"][tb] = gev
            st["g_done"] = gev

        NTL = len(tiles)
        if half == 0:
            dinfo.update(pre_d)
        else:
            stage_D(0)
            stage_D(1)
        stage_M(0)
        for n in range(NTL):
            if n + 2 < NTL:
                stage_D(n + 2)
            if n + 1 < NTL:
                stage_M(n + 1)
            stage_C1(n)
            if n >= 1:
                stage_C2(n - 1)
        stage_C2(NTL - 1)

        if STAGE == 2:
            for eng in ("pe", "act", "dve", "pool", "sp"):
                P.wait(eng, list(st["q_done"]) + [st["pe_done"][0], st["pe_done"][1]])
            return
        if half == 0:
            bg.extend(half_weight_tasks(1, cast_waits=[st["pe_done"][0], st["pe_done"][1]]))
        if half == 1:
            ph1_done = [st["pe_done"][0], st["pe_done"][1], st["gt_rd"][0], st["gt_rd"][1], st["vt_rd"][0],
                        st["vt_rd"][1], st["wcast0"], st["wcast1"], st["g_done"]] + \
                list(st["mod_done"][0] or []) + list(st["mod_done"][1] or []) + \
                list(st["mod_done"][2] or []) + list(st["q_done"])
            t_g1 = dma("p3_g1", G1, T["gate_scratch"][0:1, :].partition_broadcast(128), waits=[t_gsc])
            k_g1 = op("pool", lambda e: e.tensor_scalar(out=G1, in0=G1, scalar1=1.0, scalar2=None, op0=ALU.add),
                      waits=[t_g1])
            p3w = {"tw": {}, "wtok": {}}

            def p3_issue(k):
                sb = k % 2
                if k < 8:
                    p3w["tw"][k] = dma(f"w3st{sb}", wst3[sb], T["w_in"][k * 128:(k + 1) * 128, 0:1536],
                                       waits=ph1_done + [p3w["wtok"].get(sb)])
                else:
                    p3w["tw"][k] = dma(f"w3st{sb}", wst3[sb][:, 0:1024],
                                       T["w_out"][(k - 8) * 128:(k - 7) * 128, :], waits=[p3w["wtok"].get(sb)])

            def p3_task(k):
                sb = k % 2
                if k < 8:
                    tok = op("act", lambda e: e.activation(out=wu[:, k, :], in_=wst3[sb], func=AF.Identity),
                             waits=[p3w["tw"][k]] + ph1_done)
                    st["cu"] = tok
                else:
                    tok = op("act", lambda e: e.activation(out=wo[:, k - 8, :], in_=wst3[sb][:, 0:1024],
                                                           func=AF.Identity), waits=[p3w["tw"][k]])
                    st["co_raw"] = tok
                p3w["wtok"][sb] = tok
                if k + 2 < 16:
                    p3_issue(k + 2)

            p3_issue(0)
            p3_issue(1)
            bg.extend([lambda k=k: p3_task(k) for k in range(16)])
        pus = [(m, hp, k) for m in range(16) for hp in range(2) for k in range(m, 16)]
        NPU = len(pus)
        ready = list(st["q_done"]) + [st["g_done"], st["k_done"], st["pe_done"][0], st["pe_done"][1],
                                      st["bank_free"][4], st["bank_free"][5]]
        tk = {}

        def qk(i):
            m, hp, k = pus[i]
            par = i % 2
            for hl in range(2):
                tk[("qk", i, hl)] = op("pe", lambda e, hl=hl: e.matmul(
                    psb[par * 2 + hl][:, 0:512],
                    lhsT=qT[hl * 64:(hl + 1) * 64, hp, m * 128:(m + 1) * 128],
                    rhs=kT[hl * 64:(hl + 1) * 64, hp, k * 512:(k + 1) * 512],
                    start=True, stop=True), waits=ready + [tk.get(("sig", i - 2, hl))])

        def sig(i):
            par = i % 2
            for hl in range(2):
                tk[("sig", i, hl)] = op("act", lambda e, hl=hl: e.activation(
                    out=pbuf[par][hl], in_=psb[par * 2 + hl][:, 0:512], func=AF.Sigmoid, scale=-0.125),
                    waits=[tk[("qk", i, hl)], tk.get(("scan", i - 2, hl))])

        def scan(i):
            m, hp, k = pus[i]
            par = i % 2
            for hl in range(2):
                first = (k == m)
                w = [tk[("sig", i, hl)], tk.get(("tr", i - 3, hl)), tk.get(("scan", i - 2, hl))]
                if not first:
                    w.append(tk[("scan", i - 1, hl)])
                    init = Ebuf[(i - 1) % 3][hl][:, 511:512]
                    d1 = zeros
                else:
                    init = 0.0
                    d1 = dmat
                tk[("scan", i, hl)] = op("dve", lambda e, hl=hl, init=init, d1=d1: e.tensor_tensor_scan(
                    out=Ebuf[i % 3][hl], data0=pbuf[par][hl], data1=d1, initial=init,
                    op0=ALU.mult, op1=ALU.add), waits=w)

        def tr(i):
            par = i % 2
            for hl in range(2):
                t = None
                for blk in range(4):
                    t = op("pe", lambda e, hl=hl, blk=blk: e.transpose(
                        psbf(4 + par)[:, hl * 512 + blk * 128:hl * 512 + (blk + 1) * 128],
                        Ebuf[i % 3][hl][:, blk * 128:(blk + 1) * 128], ident),
                        waits=[tk[("scan", i, hl)], tk.get(("ev", i - 2, 0)), tk.get(("ev", i - 2, 1))],
                        sig=(blk == 3))
                tk[("tr", i, hl)] = t

        def ev(i):
            par = i % 2
            for hl in range(2):
                tk[("ev", i, hl)] = op("act", lambda e, hl=hl: e.activation(
                    out=ETbuf[par][hl], in_=psbf(4 + par)[:, hl * 512:(hl + 1) * 512], func=AF.Identity),
                    waits=[tk[("tr", i, 0)], tk[("tr", i, 1)], tk.get(("wv", i - 2))])

        def wv(i):
            m, hp, k = pus[i]
            par = i % 2
            chain = m * 2 + hp
            ob = 6 + chain % 2
            while pend and pend[0][1] <= chain - 2:
                epilogue(pend.pop(0))
            t = None
            for blk in range(4):
                for hl in range(2):
                    t = op("pe", lambda e, hl=hl, blk=blk: e.matmul(
                        psb[ob][hl * 64:(hl + 1) * 64, 0:128],
                        lhsT=g[:, k * 4 + blk, hp * 128 + hl * 64:hp * 128 + (hl + 1) * 64],
                        rhs=ETbuf[par][hl][:, blk * 128:(blk + 1) * 128],
                        start=(k == m and blk == 0), stop=(k == 15 and blk == 3)),
                        waits=[tk[("ev", i, hl)], tk.get(("ep", chain - 2))], sig=(blk == 3 and hl == 1))
            tk[("wv", i)] = t
            if k == 15:
                pend.append((i, chain, m, hp, ob, t))

        pend = []

        def epilogue(ent):
            i, chain, m, hp, ob, t = ent
            e1 = op("dve", lambda e: e.tensor_tensor(
                out=eptmp[chain % 2], in0=psb[ob][:, 0:128], in1=vown[:, hp, m * 128:(m + 1) * 128],
                op=ALU.add), waits=[t, tk.get(("ep2", chain - 2))])
            tk[("ep", chain)] = e1
            tk[("ep2", chain)] = op("dve", lambda e: e.tensor_tensor(
                out=obT[:, half * 2 + hp, m * 128:(m + 1) * 128], in0=eptmp[chain % 2],
                in1=szb[:, hp, m * 128:(m + 1) * 128], op=ALU.mult), waits=[e1])
            st["att_ep_done"] = tk[("ep2", chain)]

        def flush_ep(upto):
            while pend and pend[0][0] <= upto:
                epilogue(pend.pop(0))

        qk(0)
        if NPU > 1:
            qk(1)
        sig(0)
        for i in range(NPU):
            if i + 1 < NPU:
                sig(i + 1)
            scan(i)
            flush_ep(i - 3)
            tr(i)
            if i + 2 < NPU:
                qk(i + 2)
            ev(i)
            if bg and i % 12 == 6:
                bg.pop(0)()
            if i >= 1:
                wv(i - 1)
        wv(NPU - 1)
        flush_ep(NPU)
        while bg:
            bg.pop(0)()
        st["att_pe_done"] = tk[("wv", NPU - 1)]
        st["pe_all"] = tk[("wv", NPU - 1)]
        for b in range(8):
            st["bank_free"][b] = st["att_ep_done"] if b >= 6 else tk[("ev", NPU - 1, 1)]

        if STAGE == 3:
            for eng in ("pe", "act", "dve", "pool", "sp"):
                P.wait(eng, [st["att_ep_done"], st["att_pe_done"], tk[("ev", NPU - 1, 1)]])
            return
    fin = [st["att_ep_done"], st["att_pe_done"], tk[("ev", NPU - 1, 1)], st["wcast0"], st["wcast1"],
           st["cu"], st["co_raw"]]
    for eng in ("pe", "act", "dve", "pool", "sp"):
        P.wait(eng, fin)

    A.reset(p3_head_end)
    lng = A.alloc([1024], F32)
    lnb = A.alloc([1024], F32)
    lnvg = A.alloc([512], F32)
    lnvb = A.alloc([512], F32)
    bsT = A.alloc([4, 128], F32)
    wsT = A.alloc([8, 128], BF16)
    ws32 = A.alloc([8, 128], F32)
    sel = A.alloc([4, 128], F32, parts=8)
    bs8 = A.alloc([128], F32, parts=8)
    xs3_1 = A.alloc([8, 512], F32)
    xs3 = [xs3_1, xs3_1]
    hT3 = [A.alloc([8, 512], BF16), A.alloc([8, 512], BF16)]
    uT = [A.alloc([4, 512], F32), A.alloc([4, 512], F32)]
    zaT_1 = A.alloc([4, 512], F32)
    zaT = [zaT_1, zaT_1]
    gv = [A.alloc([512], F32) for _ in range(3)]
    vnrm = A.alloc([512], F32)
    vn = [A.alloc([512], BF16) for _ in range(3)]
    mx = A.alloc([4, 128], F32)
    oaT = [A.alloc([4, 128], BF16), A.alloc([4, 128], BF16)]
    xtok = [A.alloc([1024], F32), A.alloc([1024], F32)]
    rr = A.alloc([1024], F32)
    yo = [A.alloc([1024], F32), A.alloc([1024], F32)]
    stats = A.alloc([32], F32)

    pre_tl0 = dma("xs3", xs3[0], xT_own[:, :, 0:512])
    cu = st["cu"]
    co = None
    for kc in range(8):
        co = op("dve", lambda e, kc=kc: e.tensor_tensor(out=wo[:, kc, :], in0=wo[:, kc, :], in1=G1, op=ALU.mult),
                waits=[st["co_raw"], k_g1])
    t_lng = dma("p3_lng", lng, T["ln_g"][0:1, :].partition_broadcast(128))
    t_lnb = dma("p3_lnb", lnb, T["ln_b"][0:1, :].partition_broadcast(128))
    t_lvg = dma("p3_lvg", lnvg, T["ln_v_g"][0:1, :].partition_broadcast(128))
    t_lvb = dma("p3_lvb", lnvb, T["ln_v_b"][0:1, :].partition_broadcast(128))
    t_ws = dma("p3_ws", ws32, T["wsT"].rearrange("h j i -> j h i"))
    t_sel = dma("p3_sel", sel, T["sel"])
    t_bs = dma("p3_bs", bs8, T["bs_rev"][:, :])
    k_ws = op("dve", lambda e: e.tensor_copy(out=wsT, in_=ws32), waits=[t_ws])
    k_ws = op("dve", lambda e: e.memset(wsT[0:64, :, 64:128], 0.0), waits=[k_ws])
    tb_ = None
    for hp in range(4):
        tb_ = op("pe", lambda e, hp=hp: e.matmul(
            psb[0][:, hp * 128:(hp + 1) * 128], lhsT=sel[:, hp, :], rhs=bs8[:, :], start=True, stop=True),
            waits=[t_sel, t_bs], sig=(hp == 3))
    k_bs = op("dve", lambda e: e.tensor_copy(out=bsT, in_=psb[0][:, 0:512].rearrange("p (a b) -> p a b", b=128)),
              waits=[tb_])
    s3 = {"hT_rd": [None, None], "xs_rd": [None, None], "ps_free": [k_bs] + [None] * 7,
          "uT_rd": [None, None], "gv_rd": [None, None, None], "vn_rd": [None, None, None], "oa_rd": [None, None],
          "x_rd": [None, None], "yo_rd": [None, None], "mx_rd": None, "rr_rd": None, "vnrm_rd": None,
          "stA_rd": None, "stC_rd": None}
    out_toks = []
    tile_mt = {}
    tile_uz = {}
    blk_oa = {}
    blk_vn = {}
    sa1 = {}
    scs = {}

    tl_dma = {}

    def TLd(Tt):
        tb = Tt % 2
        if Tt == 0:
            tl_dma[0] = pre_tl0
            return
        tl_dma[Tt] = dma("xs3", xs3[tb], xT_own[:, :, Tt * 512:(Tt + 1) * 512], waits=s3["xs_rd"][0] or [])

    def TL(Tt):
        tb = Tt % 2
        if Tt not in tl_dma:
            TLd(Tt)
        t_x = tl_dma[Tt]
        mtoks = []
        for c in range(8):
            w = [t_x, s3["hT_rd"][tb]]
            mtoks.append(op("act", lambda e, c=c: e.activation(
                out=hT3[tb][:, c, :], in_=xs3[tb][:, c, :], func=AF.Identity,
                scale=sc1[:, c:c + 1], bias=sh[:, c:c + 1]), waits=w))
        s3["xs_rd"][0] = [mtoks[4], mtoks[7]]
        tile_mt[Tt] = [mtoks[4], mtoks[7]]

    tu_toks = {}

    def TUp(Tt, part):
        tb = Tt % 2
        mt = tile_mt[Tt]
        c0, dst, fn = ((0, uT[tb], AF.Gelu_apprx_tanh), (1024, zaT[tb], AF.Silu))[part // 2]
        for fc in ((part % 2) * 2, (part % 2) * 2 + 1):
            bank = fc % 2
            t = None
            for kc in range(8):
                t = op("pe", lambda e, kc=kc, fc=fc, c0=c0: e.matmul(
                    psb[bank][:, 0:512], lhsT=wu[:, kc, c0 + fc * 128:c0 + (fc + 1) * 128],
                    rhs=hT3[tb][:, kc, :], start=(kc == 0), stop=(kc == 7)),
                    waits=mt + [cu, s3["ps_free"][bank]], sig=(kc == 7))
            k = op("act", lambda e, fc=fc, dst=dst, fn=fn: e.activation(
                out=dst[:, fc, :], in_=psb[bank][:, 0:512], func=fn),
                waits=[t, s3["uT_rd"][tb], s3.get("za_rd")])
            s3["ps_free"][bank] = k
            tu_toks.setdefault(Tt, {})[(part // 2, fc)] = k
            if part >= 2:
                tile_uz[Tt] = op("pool", lambda e, fc=fc: e.tensor_tensor(
                    out=uT[tb][:, fc, :], in0=uT[tb][:, fc, :], in1=zaT[tb][:, fc, :], op=ALU.mult),
                    waits=[k, tu_toks[Tt][(0, fc)]])
                s3["za_rd"] = tile_uz[Tt]

    def TU(Tt):
        for part in range(4):
            TUp(Tt, part)

    def SA1a(B):
        Tt, blk = B // 4, B % 4
        tb = Tt % 2
        pb = B % 3
        mt = tile_mt[Tt]
        t = None
        for kc in range(8):
            t = op("pe", lambda e, kc=kc: e.matmul(
                psb[2][:, 0:512], lhsT=hT3[tb][:, kc, blk * 128:(blk + 1) * 128], rhs=wu[:, kc, 512:1024],
                start=(kc == 0), stop=(kc == 7)), waits=mt + [cu, s3["ps_free"][2]], sig=(kc == 7))
        if blk == 3:
            s3["hT_rd"][tb] = t
        k_gv = op("act", lambda e: e.activation(out=gv[pb], in_=psb[2][:, 0:512], func=AF.Gelu_apprx_tanh),
                  waits=[t, s3["gv_rd"][pb]])
        s3["ps_free"][2] = k_gv
        k_st = op("dve", lambda e: e.bn_stats(out=stats[:, 0:6], in_=gv[pb]), waits=[k_gv, s3["stA_rd"]])
        k_ag = op("dve", lambda e: e.bn_aggr(out=stats[:, 6:8], in_=stats[:, 0:6]), waits=[k_st])
        k_e = op("pool", lambda e: e.tensor_scalar(out=stats[:, 8:9], in0=stats[:, 7:8], scalar1=LN_EPS,
                                                   scalar2=None, op0=ALU.add), waits=[k_ag])
        k_sd = op("pool", lambda e: e.tensor_tensor(out=stats[:, 9:10], in0=stats[:, 8:9], in1=mhalf[:, 0:1],
                                                    op=ALU.pow), waits=[k_e])
        sa1[B] = k_sd

    def SA1b(B):
        pb = B % 3
        k_sd = sa1[B]
        k_rs = k_sd
        k_n = op("dve", lambda e: e.scalar_tensor_tensor(
            out=vnrm, in0=gv[pb], scalar=stats[:, 6:7], in1=lnvg, op0=ALU.subtract, op1=ALU.mult),
            waits=[k_rs, t_lvg, s3["vnrm_rd"]])
        s3["gv_rd"][pb] = k_n
        k_vn = op("dve", lambda e: e.scalar_tensor_tensor(
            out=vn[pb], in0=vnrm, scalar=stats[:, 9:10], in1=lnvb, op0=ALU.mult, op1=ALU.add),
            waits=[k_n, t_lvb, s3["vn_rd"][pb]])
        s3["vnrm_rd"] = k_vn
        s3["stA_rd"] = k_vn
        blk_vn[B] = k_vn

    def SA2(B):
        Tt, blk = B // 4, B % 4
        tb = Tt % 2
        pb = B % 2
        p3 = B % 3
        k_vn = blk_vn[B]
        t = None
        for hp in range(4):
            for hl in range(2):
                t = op("pe", lambda e, hp=hp, hl=hl: e.matmul(
                    psb[3][hl * 64:(hl + 1) * 64, hp * 128:(hp + 1) * 128],
                    lhsT=vn[p3][:, (2 * hp + hl) * 64:(2 * hp + hl + 1) * 64], rhs=wsT[:, 2 * hp + hl, :],
                    start=True, stop=True), waits=[k_vn, k_ws, s3["ps_free"][3]],
                    sig=(hp == 3 and hl == 1))
        s3["vn_rd"][p3] = t
        k_mx = op("dve", lambda e: e.tensor_tensor(
            out=mx[:, :, :], in0=psb[3][:, 0:512].rearrange("p (a b) -> p a b", b=128), in1=bsT[:, :, :],
            op=ALU.add), waits=[t, k_bs, s3["mx_rd"]])
        s3["ps_free"][3] = k_mx
        k_oa = op("dve", lambda e: e.tensor_tensor(
            out=oaT[pb][:, :, :], in0=mx[:, :, :], in1=uT[tb][:, :, blk * 128:(blk + 1) * 128], op=ALU.mult),
            waits=[k_mx, tile_uz[Tt], s3["oa_rd"][pb]])
        s3["mx_rd"] = k_oa
        if blk == 3:
            s3["uT_rd"][tb] = k_oa
        blk_oa[B] = k_oa

    def SCa(B):
        pb = B % 2
        k_oa = blk_oa[B]
        ty = [None, None]
        for nh in range(2):
            bank = 4 + nh
            t = None
            for fc in range(8):
                lhs = oaT[pb][:, fc, :] if fc < 4 else obT[:, fc - 4, B * 128:(B + 1) * 128]
                t = op("pe", lambda e, fc=fc, nh=nh, lhs=lhs: e.matmul(
                    psb[bank][:, 0:512], lhsT=lhs, rhs=wo[:, fc, nh * 512:(nh + 1) * 512],
                    start=(fc == 0), stop=(fc == 7)), waits=[k_oa, co, s3["ps_free"][bank]], sig=(fc == 7))
            ty[nh] = t
        s3["oa_rd"][pb] = ty[1]
        t_xt = dma(f"xtok{pb}", xtok[pb], T["x_own"][B * 128:(B + 1) * 128, :], waits=[s3["x_rd"][pb]])
        k_r = None
        k_s = None
        for nh in range(2):
            k_r = op("dve", lambda e, nh=nh: e.scalar_tensor_tensor(
                out=rr[:, nh * 512:(nh + 1) * 512], in0=xtok[pb][:, nh * 512:(nh + 1) * 512], scalar=ALPHA,
                in1=psb[4 + nh][:, 0:512], op0=ALU.mult, op1=ALU.add), waits=[ty[nh], t_xt, s3["rr_rd"]])
            s3["ps_free"][4 + nh] = k_r
            k_s = op("dve", lambda e, nh=nh: e.bn_stats(out=stats[:, 12 + nh * 6:18 + nh * 6],
                                                        in_=rr[:, nh * 512:(nh + 1) * 512]),
                     waits=[k_r, s3["stC_rd"]])
        s3["x_rd"][pb] = k_r
        k_ag = op("dve", lambda e: e.bn_aggr(out=stats[:, 24:26], in_=stats[:, 12:24]), waits=[k_s])
        k_e = op("pool", lambda e: e.tensor_scalar(out=stats[:, 26:27], in0=stats[:, 25:26], scalar1=LN_EPS,
                                                   scalar2=None, op0=ALU.add), waits=[k_ag])
        k_sd = op("pool", lambda e: e.tensor_tensor(out=stats[:, 27:28], in0=stats[:, 26:27], in1=mhalf[:, 0:1],
                                                    op=ALU.pow), waits=[k_e])
        scs[B] = k_sd

    def SCb(B):
        pb = B % 2
        k_sd = scs[B]
        k_y = op("dve", lambda e: e.scalar_tensor_tensor(
            out=yo[pb], in0=rr, scalar=stats[:, 24:25], in1=lng, op0=ALU.subtract, op1=ALU.mult),
            waits=[k_sd, t_lng, s3["yo_rd"][pb]])
        s3["rr_rd"] = k_y
        s3["stC_rd"] = k_y
        k_y3 = op("dve", lambda e: e.scalar_tensor_tensor(
            out=yo[pb], in0=yo[pb], scalar=stats[:, 27:28], in1=lnb, op0=ALU.mult, op1=ALU.add),
            waits=[k_y, t_lnb])
        s3["stC_rd"] = k_y3
        t_o = dma(f"yout{pb}", T["out_own"][B * 128:(B + 1) * 128, :], yo[pb], waits=[k_y3])
        s3["yo_rd"][pb] = t_o
        out_toks.append(t_o)

    TL(0)
    TL(1)
    TU(0)
    SA1a(0)
    SA1b(0)
    SA1a(1)
    SA1b(1)
    for B in range(16):
        Tt, blk = B // 4, B % 4
        if blk == 0 and Tt + 2 < 4:
            TLd(Tt + 2)
        if Tt + 1 < 4:
            TUp(Tt + 1, blk)
        if B + 2 < 16:
            SA1a(B + 2)
        SA2(B)
        if B >= 1:
            SCa(B - 1)
        if B + 2 < 16:
            SA1b(B + 2)
        if blk == 2 and Tt + 2 < 4:
            TL(Tt + 2)
        if B >= 1:
            SCb(B - 1)
    SCa(15)
    SCb(15)
    P.wait("sp", out_toks[-2:])
    P.wait("act", out_toks[-2:])


_IN_SPECS = [
    ("xT_all", [DM, S + 1]), ("xT_own", [DM, NOWN]), ("x_own", [NOWN, DM]),
    ("w_in", [DM, 3584]), ("w_out", [DM, DM]), ("w_ada", [DM, 3072]), ("b_ada", [1, 3072]),
    ("cT", [128, 8]), ("ln_v_g", [1, 512]), ("ln_v_b", [1, 512]), ("ln_g", [1, DM]), ("ln_b", [1, DM]),
    ("wsT", [8, 128, 128]), ("bs_rev", [8, 128]), ("sel", [8, 4, 128]), ("dmat", [128, 512]),
    ("ident", [128, 128]),
]


def build_nc():
    nc = bass.Bass("TRN2", target_bir_lowering=False)
    T = {}
    for name, shape in _IN_SPECS:
        T[name] = nc.dram_tensor(name, shape, F32, kind="ExternalInput").ap()
    T["out_own"] = nc.dram_tensor("out_own", [NOWN, DM], F32, kind="ExternalOutput").ap()
    T["gate_scratch"] = nc.dram_tensor("gate_scratch", [1, DM], F32, kind="Internal").ap()
    with contextlib.ExitStack() as es:
        arena = es.enter_context(nc.sbuf_tensor("arena", [128, ARENA_WORDS], F32))
        psb = [es.enter_context(nc.psum_tensor(f"psb{i}", [128, 512], F32)) for i in range(8)]
        sems = {k: es.enter_context(nc.semaphore(f"s_{k}")) for k in ("pe", "act", "dve", "pool")}
        dsems = [es.enter_context(nc.semaphore(f"d_{i}")) for i in range(N_DSEM)]
        block = es.enter_context(nc.Block())
        arena_ap = arena[:, :]
        psb_ap = [p[:, :] for p in psb]

        @block.tensor
        def _(e):
            generate(Prog("pe", e, sems, dsems), nc, T, arena_ap, psb_ap)

        @block.scalar
        def _(e):
            generate(Prog("act", e, sems, dsems), nc, T, arena_ap, psb_ap)

        @block.vector
        def _(e):
            generate(Prog("dve", e, sems, dsems), nc, T, arena_ap, psb_ap)

        @block.gpsimd
        def _(e):
            generate(Prog("pool", e, sems, dsems), nc, T, arena_ap, psb_ap)

        @block.sync
        def _(e):
            generate(Prog("sp", e, sems, dsems), nc, T, arena_ap, psb_ap)
    return nc


def _own_idx(j):
    return (np.arange(16)[:, None] * 512 + 128 * j + np.arange(128)[None, :]).reshape(-1)


def kernel(x, c, w_ada, b_ada, w_in, ln_v_g, ln_v_b, w_spatial, b_spatial, w_out, ln_g, ln_b):
    f = lambda a: np.ascontiguousarray(np.asarray(a, dtype=np.float32))
    x = f(x); c = f(c)
    wsT = f(np.transpose(f(w_spatial)[0][:, ::-1, ::-1], (0, 2, 1)))
    bs_rev = f(f(b_spatial)[0][:, ::-1])
    sel = np.zeros((8, 4, 128), np.float32)
    for h in range(8):
        sel[h, h // 2, (h % 2) * 64:(h % 2) * 64 + 64] = 1.0
    ident = np.eye(128, dtype=np.float32)
    shared = {
        "w_in": f(w_in)[0], "w_out": f(w_out)[0], "w_ada": f(w_ada)[0], "b_ada": f(b_ada)[0][None, :],
        "ln_v_g": f(ln_v_g)[0][None, :], "ln_v_b": f(ln_v_b)[0][None, :],
        "ln_g": f(ln_g)[0][None, :], "ln_b": f(ln_b)[0][None, :],
        "wsT": wsT, "bs_rev": bs_rev, "sel": sel, "ident": ident,
    }
    in_maps = []
    for core in range(8):
        b, j = core // 4, core % 4
        xr = x[b, ::-1, :]
        xT_all = np.zeros((DM, S + 1), np.float32)
        xT_all[:, :S] = xr.T
        idx = _own_idx(j)
        x_own = f(xr[idx])
        dmat = np.zeros((128, 512), np.float32)
        dmat[np.arange(128), 128 * j + np.arange(128)] = 1.0
        m = dict(shared)
        m.update({"xT_all": xT_all, "xT_own": f(x_own.T), "x_own": x_own,
                  "cT": f(c[b].reshape(8, 128).T), "dmat": dmat})
        in_maps.append(m)
    nc = build_nc()
    res = run_bass_kernel_spmd(nc, in_maps, core_ids=list(range(8)))
    out = np.zeros((2, S, DM), np.float32)
    for core in range(8):
        b, j = core // 4, core % 4
        o_rev = res.results[core]["out_own"]
        out[b, S - 1 - _own_idx(j), :] = o_rev
    return out
```
